# Optimizing a Trainium2 kernel written in Bass

```python
import jax, jax.numpy as jnp
from jax import lax
import numpy as np

D_MODEL = 1024
BATCH = 8
SEQ = 8192
DEPTH = 4

EXPAND = 2
D_BRANCH = EXPAND * D_MODEL
HEAD_DIM = 64
N_HEADS = D_BRANCH // HEAD_DIM
N_MIXERS = 3
N_LAYERS_A = (DEPTH + 2) // 3
N_LAYERS_B = (DEPTH + 1) // 3
N_LAYERS_C = DEPTH // 3
RMS_EPS = 1e-5
BLOCK = 128

POOL_WINDOWS = (2, 4, 8, 16)
N_POOL_GROUPS = len(POOL_WINDOWS)
POOL_GROUP_DIM = D_BRANCH // N_POOL_GROUPS
A_IN_WIDTH = 2 * D_BRANCH

SWA_WINDOW = 128
SWA_KV_HEADS = N_HEADS // 8
SWA_GROUP = N_HEADS // SWA_KV_HEADS
KV_WIDTH = SWA_KV_HEADS * HEAD_DIM
B_IN_WIDTH = 2 * D_BRANCH + 2 * KV_WIDTH

DILATED_PAIRS = ((128, 1), (512, 4), (2048, 16))
N_DIL_GROUPS = len(DILATED_PAIRS)
C_IN_WIDTH = (3 * N_DIL_GROUPS + 1) * D_BRANCH

kernel_name = "hybrid_pool_swa_dilated_gated_trunk"


def _rmsnorm(x, g):
    x32 = x.astype(jnp.float32)
    y = x32 * lax.rsqrt(jnp.mean(x32 * x32, axis=-1, keepdims=True) + RMS_EPS)
    return (y * g.astype(jnp.float32)).astype(x.dtype)


def _banded_attention(q, k, v, max_dist, sink):
    r_n, n, seq_len, hk, grp, hd = q.shape
    n_blk = -(-seq_len // BLOCK)
    pad_end = n_blk * BLOCK - seq_len
    qp = jnp.pad(q, ((0, 0), (0, 0), (0, pad_end), (0, 0), (0, 0), (0, 0)))
    kv_pad = ((0, 0), (0, 0), (BLOCK, pad_end), (0, 0), (0, 0))
    kp = jnp.pad(k, kv_pad)
    vp = jnp.pad(v, kv_pad)
    scale = HEAD_DIM ** -0.5

    def block(idx):
        r = idx // n_blk
        start = (idx % n_blk) * BLOCK
        qr = lax.dynamic_index_in_dim(qp, r, 0, keepdims=False)
        kr = lax.dynamic_index_in_dim(kp, r, 0, keepdims=False)
        vr = lax.dynamic_index_in_dim(vp, r, 0, keepdims=False)
        qb = lax.dynamic_slice_in_dim(qr, start, BLOCK, axis=1).astype(jnp.float32)
        kb = lax.dynamic_slice_in_dim(kr, start, 2 * BLOCK, axis=1).astype(jnp.float32)
        vb = lax.dynamic_slice_in_dim(vr, start, 2 * BLOCK, axis=1).astype(jnp.float32)
        s = jnp.einsum('nqkgd,nskd->nkgqs', qb, kb) * scale
        q_pos = start + jnp.arange(BLOCK)
        k_pos = start - BLOCK + jnp.arange(2 * BLOCK)
        dist = q_pos[:, None] - k_pos[None, :]
        valid = (dist >= 0) & (dist <= max_dist) & (k_pos >= 0)[None, :]
        s = jnp.where(valid, s, -jnp.inf)
        lse = jax.nn.logsumexp(s, axis=-1)
        if sink is not None:
            lse = jnp.logaddexp(lse, sink.astype(jnp.float32)[None, :, :, None])
        p = jnp.exp(s - lse[..., None])
        o = jnp.einsum('nkgqs,nskd->nqkgd', p, vb)
        return o, lse

    o, lse = lax.map(block, jnp.arange(r_n * n_blk))
    o = o.reshape(r_n, n_blk, n, BLOCK, hk, grp, hd)
    o = jnp.transpose(o, (0, 2, 1, 3, 4, 5, 6)).reshape(r_n, n, n_blk * BLOCK, hk, grp, hd)[:, :, :seq_len]
    lse = lse.reshape(r_n, n_blk, n, hk, grp, BLOCK)
    lse = jnp.transpose(lse, (0, 2, 1, 5, 3, 4)).reshape(r_n, n, n_blk * BLOCK, hk, grp)[:, :, :seq_len]
    return o, lse


def _pool_mixer(u, w_group, scale):
    bsz, seq, _ = u.shape
    ug = u.reshape(bsz, seq, N_POOL_GROUPS, POOL_GROUP_DIM)
    c = jnp.cumsum(ug.astype(jnp.float32), axis=1)
    t = jnp.arange(seq)
    pooled = []
    for gi, w in enumerate(POOL_WINDOWS):
        cg = c[:, :, gi]
        shifted = jnp.pad(cg, ((0, 0), (w, 0), (0, 0)))[:, :seq]
        count = jnp.minimum(t + 1, w).astype(jnp.float32)
        pooled.append((cg - shifted) / count[None, :, None])
    d = (jnp.stack(pooled, axis=2) - ug.astype(jnp.float32)).astype(u.dtype)
    y = jnp.einsum('bsgc,gcd->bsgd', d, w_group)
    return y.reshape(bsz, seq, D_BRANCH) * scale


def _swa_mixer(p, sinks):
    bsz, seq, _ = p.shape
    q, k, v, gate = jnp.split(p, [D_BRANCH, D_BRANCH + KV_WIDTH, D_BRANCH + 2 * KV_WIDTH], axis=-1)
    q = q.reshape(1, bsz, seq, SWA_KV_HEADS, SWA_GROUP, HEAD_DIM)
    k = k.reshape(1, bsz, seq, SWA_KV_HEADS, HEAD_DIM)
    v = v.reshape(1, bsz, seq, SWA_KV_HEADS, HEAD_DIM)
    o, _ = _banded_attention(q, k, v, SWA_WINDOW - 1, sinks.reshape(SWA_KV_HEADS, SWA_GROUP))
    return o[0].reshape(bsz, seq, D_BRANCH).astype(p.dtype), gate


def _to_residues(a, d):
    bsz, seq = a.shape[:2]
    a = a.reshape((bsz, seq // d, d) + a.shape[2:])
    return jnp.moveaxis(a, 2, 0)


def _from_residues(a):
    d, bsz, l = a.shape[:3]
    a = jnp.moveaxis(a, 0, 2)
    return a.reshape((bsz, l * d) + a.shape[3:])


def _dilated_mixer(h, w_in):
    bsz, seq, _ = h.shape

    def proj(c):
        return jnp.einsum('bsd,de->bse', h, w_in[:, c * D_BRANCH:(c + 1) * D_BRANCH])

    outs, lses = [], []
    for gi, (window, dil) in enumerate(DILATED_PAIRS):
        q, k, v = (proj(3 * gi + j).reshape(bsz, seq, N_HEADS, HEAD_DIM) for j in range(3))
        o, lse = _banded_attention(_to_residues(q, dil)[..., None, :], _to_residues(k, dil),
                                   _to_residues(v, dil), window // dil, None)
        outs.append(_from_residues(o[..., 0, :]))
        lses.append(_from_residues(lse[..., 0]))
    wts = jax.nn.softmax(jnp.stack(lses, 0), axis=0)
    o = jnp.einsum('gbsh,gbshd->bshd', wts, jnp.stack(outs, 0))
    return o.reshape(bsz, seq, D_BRANCH).astype(h.dtype), proj(3 * N_DIL_GROUPS)


def setup_inputs(seed: int = 0) -> dict:
    key = jax.random.key(seed)
    ks = jax.random.split(key, 11)
    f32 = jnp.float32
    nrm = jax.random.normal
    return {
        "x": nrm(ks[0], (BATCH, SEQ, D_MODEL), f32),
        "norm_g": 1.0 + 0.05 * nrm(ks[1], (DEPTH, D_MODEL), f32),
        "final_g": 1.0 + 0.05 * nrm(ks[2], (D_MODEL,), f32),
        "w_out": nrm(ks[3], (DEPTH, D_BRANCH, D_MODEL), f32) * D_BRANCH ** -0.5,
        "a_w_in": nrm(ks[4], (N_LAYERS_A, D_MODEL, A_IN_WIDTH), f32) * D_MODEL ** -0.5,
        "a_w_group": nrm(ks[5], (N_LAYERS_A, N_POOL_GROUPS, POOL_GROUP_DIM, POOL_GROUP_DIM), f32) * POOL_GROUP_DIM ** -0.5,
        "a_scale": 1.0 + 0.1 * nrm(ks[6], (N_LAYERS_A, D_BRANCH), f32),
        "b_w_in": nrm(ks[7], (N_LAYERS_B, D_MODEL, B_IN_WIDTH), f32) * D_MODEL ** -0.5,
        "b_sinks": 0.5 * nrm(ks[8], (N_LAYERS_B, N_HEADS), f32),
        "c_w_in": nrm(ks[9], (N_LAYERS_C, D_MODEL, C_IN_WIDTH), f32) * D_MODEL ** -0.5,
    }


def reference(x, norm_g, final_g, w_out, a_w_in, a_w_group, a_scale, b_w_in, b_sinks, c_w_in):
    for i in range(DEPTH):
        h = _rmsnorm(x, norm_g[i])
        kind, j = i % N_MIXERS, i // N_MIXERS
        if kind == 0:
            p = jnp.einsum('bsd,de->bse', h, a_w_in[j])
            u, gate = jnp.split(p, 2, axis=-1)
            y = _pool_mixer(u, a_w_group[j], a_scale[j])
        elif kind == 1:
            y, gate = _swa_mixer(jnp.einsum('bsd,de->bse', h, b_w_in[j]), b_sinks[j])
        else:
            y, gate = _dilated_mixer(h, c_w_in[j])
        x = x + jnp.einsum('bse,ed->bsd', y * jax.nn.silu(gate), w_out[i])
    return _rmsnorm(x, final_g)
```

```python
import numpy as np
from contextlib import ExitStack
import concourse.bass as bass
import concourse.mybir as mybir
from concourse.bass_utils import run_bass_kernel_spmd

F32 = mybir.dt.float32
BF16 = mybir.dt.bfloat16
AF = mybir.ActivationFunctionType
ALU = mybir.AluOpType
AX = mybir.AxisListType

T = 8192
DM = 1024
DB = 2048
NCH = 16
KD = 8
EPS = 1e-5
N_CORES = 8


class Buf:
    __slots__ = ("w", "r")

    def __init__(self):
        self.w = {}
        self.r = {}


class Prog:
    STREAMS = ("sp", "act", "pe", "dve", "pool")
    COMPUTE = ("act", "pe", "dve", "pool")
    DMAQ = ("sp", "pool")
    NDMA = 24

    def __init__(self, nc, es):
        self.nc = nc
        self.ops = {s: [] for s in self.STREAMS}
        self.idx = {s: 0 for s in self.STREAMS}
        self.seen = {s: {} for s in self.STREAMS}
        self.done = {}
        self.cnt = {}
        for s in self.COMPUTE:
            self.done[s] = es.enter_context(nc.semaphore("done_" + s))
            self.cnt[s] = 0
        self.dsem = {q: [es.enter_context(nc.semaphore("d%s%d" % (q, i))) for i in range(self.NDMA)]
                     for q in self.DMAQ}
        self.dcnt = {q: [0] * self.NDMA for q in self.DMAQ}
        self.drr = {q: 0 for q in self.DMAQ}

    def _waits(self, stream, reads, writes):
        need = {}
        seen = self.seen[stream]
        cur = self.idx[stream]

        def add(tok):
            key, sem, val, eng, idx = tok
            if eng == stream:
                if stream == "pe":
                    return
                if cur - idx > 2:
                    return
            if seen.get(key, 0) >= val:
                return
            if key not in need or need[key][1] < val:
                need[key] = (sem, val)

        for b in reads:
            for t in b.w.values():
                add(t)
        for b in writes:
            for t in b.w.values():
                add(t)
            for t in b.r.values():
                add(t)
        for key, (sem, val) in need.items():
            seen[key] = val
        return list(need.values())

    def _update(self, tok, reads, writes):
        key = tok[0]
        for b in writes:
            b.w = {key: tok}
            b.r = {}
        for b in reads:
            if b not in writes:
                b.r[key] = tok

    def op(self, stream, fn, reads=(), writes=(), mark=True):
        waits = self._waits(stream, reads, writes)
        i = self.idx[stream]
        self.idx[stream] = i + 1
        if mark:
            self.cnt[stream] += 1
            val = self.cnt[stream]
        else:
            val = self.cnt[stream] + 1
        sem = self.done[stream]
        tok = ("E" + stream, sem, val, stream, i)

        def run(e, waits=waits, fn=fn, mark=mark, sem=sem):
            for (s, v) in waits:
                e.wait_ge(s, v)
            ins = fn(e)
            if mark:
                ins.then_inc(sem, 1)

        self.ops[stream].append(run)
        self._update(tok, reads, writes)
        return tok

    def dma(self, q, out, in_, reads=(), writes=(), slow=False):
        waits = self._waits(q, reads, writes)
        k = self.drr[q]
        self.drr[q] = (k + 1) % self.NDMA
        sem = self.dsem[q][k]
        prev = self.dcnt[q][k]
        key = "D%s%d" % (q, k)
        if prev > 0 and self.seen[q].get(key, 0) < prev:
            waits.append((sem, prev))
            self.seen[q][key] = prev
        self.dcnt[q][k] = prev + 16
        i = self.idx[q]
        self.idx[q] = i + 1
        tok = (key, sem, prev + 16, None, i)

        def run(e, waits=waits, out=out, in_=in_, sem=sem, slow=slow):
            for (s, v) in waits:
                e.wait_ge(s, v)
            if slow:
                e.dma_start(out=out, in_=in_, allow_slow_non_contiguous=True).then_inc(sem, 16)
            else:
                e.dma_start(out=out, in_=in_).then_inc(sem, 16)

        self.ops[q].append(run)
        self._update(tok, reads, writes)
        return tok

    def barrier(self):
        targets = []
        for E in self.COMPUTE:
            if self.cnt[E] > 0:
                targets.append(("E" + E, self.done[E], self.cnt[E]))
        for q in self.DMAQ:
            for k in range(self.NDMA):
                if self.dcnt[q][k] > 0:
                    targets.append(("D%s%d" % (q, k), self.dsem[q][k], self.dcnt[q][k]))
        for s in self.STREAMS:
            waits = []
            for key, sem, val in targets:
                if self.seen[s].get(key, 0) < val:
                    waits.append((sem, val))
                    self.seen[s][key] = val

            def run(e, waits=waits):
                for (sm, v) in waits:
                    e.wait_ge(sm, v)

            self.ops[s].append(run)

    def run_all(self, block):
        ops = self.ops

        @block.sync
        def _(e):
            for f in ops["sp"]:
                f(e)

        @block.scalar
        def _(e):
            for f in ops["act"]:
                f(e)

        @block.tensor
        def _(e):
            for f in ops["pe"]:
                f(e)

        @block.vector
        def _(e):
            for f in ops["dve"]:
                f(e)

        @block.gpsimd
        def _(e):
            for f in ops["pool"]:
                f(e)


class Arena:
    def __init__(self, nc, base, limit):
        self.nc = nc
        self.off = base
        self.limit = limit
        self.n = 0
        self.cache = {}

    def alloc(self, shape, dtype):
        nb = mybir.dt.size(dtype)
        for s in shape[1:]:
            nb *= s
        nb = (nb + 63) // 64 * 64
        key = (self.off, tuple(shape), str(dtype))
        if key not in self.cache:
            self.cache[key] = self.nc.alloc_sbuf_tensor_at("ar%d" % self.n, list(shape), dtype, offset=self.off)
            self.n += 1
        t = self.cache[key]
        self.off += nb
        assert self.off <= self.limit, ("SBUF arena overflow", self.off)
        return t

    def mark(self):
        return self.off

    def release(self, m):
        self.off = m


def build_program(layers=(0, 1, 2, 3), final_norm=True, tokens=8192):
    global T
    T = tokens
    nc = bass.Bass("TRN2", target_bir_lowering=False)
    es = ExitStack()
    x_in = nc.dram_tensor("x", [T, DM], F32, kind="ExternalInput").ap()
    norm_g = nc.dram_tensor("norm_g", [4, DM], F32, kind="ExternalInput").ap()
    final_g = nc.dram_tensor("final_g", [1, DM], F32, kind="ExternalInput").ap()
    w_out = nc.dram_tensor("w_out", [4, DB, DM], F32, kind="ExternalInput").ap()
    a_w_in = nc.dram_tensor("a_w_in", [2, DM, 2 * DB], F32, kind="ExternalInput").ap()
    a_w_group = nc.dram_tensor("a_w_group", [2, 4, 512, 512], F32, kind="ExternalInput").ap()
    a_scale = nc.dram_tensor("a_scale", [2, DB], F32, kind="ExternalInput").ap()
    b_w_in = nc.dram_tensor("b_w_in", [1, DM, 4608], F32, kind="ExternalInput").ap()
    b_sinks = nc.dram_tensor("b_sinks", [1, 32], F32, kind="ExternalInput").ap()
    c_w_in = nc.dram_tensor("c_w_in", [1, DM, 20480], F32, kind="ExternalInput").ap()
    y_out = nc.dram_tensor("out", [T, DM], F32, kind="ExternalOutput").ap()
    xs = [nc.dram_tensor("xs%d" % i, [T, DM], F32, kind="Internal").ap() for i in range(2)]
    hT = nc.dram_tensor("hT", [128, KD, T], BF16, kind="Internal").ap()
    zT = nc.dram_tensor("zT", [128, NCH, T], BF16, kind="Internal").ap()

    P = Prog(nc, es)
    rem = nc.sbuf_bytes_remaining - 4096
    resv = nc.alloc_sbuf_tensor("arena_resv", [128, rem], mybir.dt.uint8)
    abase = nc.lookup_mloc(resv).addr
    A = Arena(nc, abase, abase + rem)
    banks = [nc.alloc_psum_tensor("bank%d" % i, [128, 512], F32) for i in range(8)]
    bankb = [Buf() for _ in range(8)]

    ident = A.alloc([128, 128], BF16)
    identb = Buf()
    tmpf = A.alloc([128, 256], F32)
    tmpfb = Buf()
    mask_b = A.alloc([128, 256], BF16)
    mask_c = A.alloc([128, 256], BF16)
    maskb = Buf()
    gT = A.alloc([128, 4, KD], F32)
    gTb = Buf()
    scT = A.alloc([128, 2, NCH], F32)
    scTb = Buf()
    esk = A.alloc([128, 32], F32)
    eskb = Buf()
    rc16 = A.alloc([128, 16], F32)
    rcw = A.alloc([128, 4, 16], F32)
    rcb = Buf()
    epst = A.alloc([128, 1], F32)
    epsb = Buf()
    P.op("pool", lambda e: e.memset(epst[:, :], EPS), writes=[epsb])

    P.op("pool", lambda e: e.memset(tmpf[:, 0:128], 0.0), writes=[tmpfb])
    P.op("pool", lambda e: e.affine_select(out=tmpf[:, 0:128], in_=tmpf[:, 0:128], pattern=[[-1, 128]],
                                           compare_op=ALU.not_equal, fill=1.0, base=0, channel_multiplier=1),
         writes=[tmpfb])
    P.op("pool", lambda e: e.tensor_copy(out=ident[:, :], in_=tmpf[:, 0:128]), reads=[tmpfb], writes=[identb])
    for (mt, prev_op) in ((mask_b, ALU.is_gt), (mask_c, ALU.is_ge)):
        P.op("pool", lambda e: e.memset(tmpf[:, :], 1.0), reads=[], writes=[tmpfb])
        P.op("pool", lambda e, prev_op=prev_op: e.affine_select(
            out=tmpf[:, 0:128], in_=tmpf[:, 0:128], pattern=[[-1, 128]], compare_op=prev_op, fill=0.0,
            base=0, channel_multiplier=1), writes=[tmpfb])
        P.op("pool", lambda e: e.affine_select(
            out=tmpf[:, 128:256], in_=tmpf[:, 128:256], pattern=[[1, 128]], compare_op=ALU.is_ge, fill=0.0,
            base=0, channel_multiplier=-1), writes=[tmpfb])
        P.op("pool", lambda e, mt=mt: e.tensor_copy(out=mt[:, :], in_=tmpf[:, :]), reads=[tmpfb], writes=[maskb])
    P.dma("sp", gT[:, :, :], norm_g.rearrange("l (c p) -> p l c", p=128), writes=[gTb], slow=True)
    P.dma("sp", scT[:, :, :], a_scale.rearrange("l (c p) -> p l c", p=128), writes=[scTb], slow=True)
    P.dma("sp", esk[:, :], b_sinks[0:1, :].partition_broadcast(128), writes=[eskb], slow=True)
    P.op("act", lambda e: e.activation(out=esk[:, :], in_=esk[:, :], func=AF.Exp), writes=[eskb])
    for t in range(16):
        P.op("pool", lambda e, t=t: e.memset(rc16[:, t:t + 1], 1.0 / (t + 1)), writes=[rcb])
    for gi in range(4):
        P.op("pool", lambda e, gi=gi: e.tensor_scalar(out=rcw[:, gi, :], in0=rc16[:, :],
                                                      scalar1=1.0 / (2 << gi), scalar2=None, op0=ALU.max),
             writes=[rcb])
    esk2 = A.alloc([128, NCH], F32)
    P.op("pool", lambda e: e.tensor_copy(out=esk2[0:64, :], in_=esk[0:64, 0:32:2]), reads=[eskb], writes=[eskb])
    P.op("pool", lambda e: e.tensor_copy(out=esk2[64:128, :], in_=esk[64:128, 1:32:2]), reads=[eskb], writes=[eskb])
    base_mark = A.mark()

    PB = (0, 1, 2)
    SBK = (3, 4, 5)
    OBK = (6, 7)
    pctr = [0]

    def next_pbank():
        b = PB[pctr[0] % 3]
        pctr[0] += 1
        return b

    class AttnPipe:
        DEPTH = 2

        def __init__(self, E, Eb, mask2):
            self.E, self.Eb, self.mask2 = E, Eb, mask2
            self.ns = 0
            self.pending = []

        def push(self, units, kreads, vreads, o_cb):
            sidx = self.ns
            self.ns += 1
            bi = SBK[sidx % 3]
            nu = len(units)
            for u, un in enumerate(units):
                kp = un["kprev"] if un["kprev"] is not None else un["kcur"]
                P.op("pe", lambda e, bi=bi, u=u, un=un, kp=kp: e.matmul(
                    banks[bi][:, u * 256:u * 256 + 128], kp, un["q2"][:, u, :], start=True, stop=True,
                    skip_group_check=True), reads=kreads, writes=[bankb[bi]], mark=False)
                P.op("pe", lambda e, bi=bi, u=u, un=un: e.matmul(
                    banks[bi][:, u * 256 + 128:u * 256 + 256], un["kcur"], un["q2"][:, u, :], start=True, stop=True,
                    skip_group_check=True), reads=kreads, writes=[bankb[bi]], mark=(u == nu - 1))
            ei = sidx % len(self.E)
            E_t, E_b = self.E[ei], self.Eb[ei]
            P.op("act", lambda e, bi=bi, E_t=E_t: e.activation(out=E_t[:, :], in_=banks[bi][:, :], func=AF.Exp,
                                                            scale=0.125),
                 reads=[bankb[bi]], writes=[E_b])
            P.op("dve", lambda e, E_t=E_t: e.tensor_tensor(
                out=E_t[:, :].rearrange("p (u c) -> p u c", u=2), in0=E_t[:, :].rearrange("p (u c) -> p u c", u=2),
                in1=self.mask2[:, :].unsqueeze(1).to_broadcast([128, 2, 256]), op=ALU.mult),
                reads=[maskb], writes=[E_b])
            self.pending.append((sidx, units, ei, vreads, o_cb))
            if len(self.pending) > self.DEPTH:
                self._pv(self.pending.pop(0))

        def _pv(self, item):
            sidx, units, ei, vreads, o_cb = item
            oi = OBK[(sidx // 2) % 2]
            half = sidx % 2
            E_t, E_b = self.E[ei], self.Eb[ei]
            nu = len(units)
            for u, un in enumerate(units):
                col = half * 256 + u * 128
                last = (u == nu - 1)
                if un["vprev"] is not None:
                    P.op("pe", lambda e, oi=oi, col=col, u=u, un=un, E_t=E_t: e.matmul(
                        banks[oi][:, col:col + 128], un["vprev"], E_t[:, u * 256:u * 256 + 128], start=True,
                        stop=False, skip_group_check=True),
                        reads=[E_b] + vreads, writes=[bankb[oi]], mark=False)
                    P.op("pe", lambda e, oi=oi, col=col, u=u, un=un, E_t=E_t: e.matmul(
                        banks[oi][:, col:col + 128], un["vcur"], E_t[:, u * 256 + 128:u * 256 + 256], start=False,
                        stop=True, skip_group_check=True),
                        reads=[E_b] + vreads, writes=[bankb[oi]], mark=last)
                else:
                    P.op("pe", lambda e, oi=oi, col=col, u=u, un=un, E_t=E_t: e.matmul(
                        banks[oi][:, col:col + 128], un["vcur"], E_t[:, u * 256 + 128:u * 256 + 256], start=True,
                        stop=True, skip_group_check=True),
                        reads=[E_b] + vreads, writes=[bankb[oi]], mark=last)
            if o_cb is not None:
                o_cb(oi)

        def flush(self):
            while self.pending:
                self._pv(self.pending.pop(0))

    def finalize(UA, UB, Ub, rec, recb, sg_t, sg_b, zs_t, zs_b, N, sinkcol):
        f0 = AF.Ln if sinkcol is None else AF.Copy
        P.op("act", lambda e: e.activation(out=rec[0:64, 0:N], in_=UA[64:128, 0:N], func=f0),
             reads=[Ub], writes=[recb])
        P.op("act", lambda e: e.activation(out=rec[64:128, 0:N], in_=UB[0:64, 0:N], func=f0),
             reads=[Ub], writes=[recb])
        if sinkcol is not None:
            P.op("dve", lambda e: e.tensor_scalar(out=rec[:, 0:N], in0=rec[:, 0:N],
                                                  scalar1=esk2[:, sinkcol:sinkcol + 1], scalar2=None, op0=ALU.add),
                 reads=[eskb], writes=[recb])
            P.op("act", lambda e: e.activation(out=rec[:, 0:N], in_=rec[:, 0:N], func=AF.Ln), writes=[recb])
        P.op("act", lambda e: e.activation(out=rec[:, 0:N], in_=rec[:, 0:N], func=AF.Exp, scale=-1.0),
             writes=[recb])
        P.op("pool", lambda e: e.tensor_tensor(out=rec[0:64, 0:N], in0=UA[0:64, 0:N], in1=rec[0:64, 0:N],
                                               op=ALU.mult), reads=[Ub], writes=[recb])
        P.op("pool", lambda e: e.tensor_tensor(out=rec[64:128, 0:N], in0=UB[64:128, 0:N], in1=rec[64:128, 0:N],
                                               op=ALU.mult), reads=[Ub], writes=[recb])
        P.op("pool", lambda e: e.tensor_tensor(out=zs_t[:, 0:N], in0=rec[:, 0:N], in1=sg_t[:, 0:N],
                                               op=ALU.mult), reads=[recb, sg_b], writes=[zs_b])

    def proj_fm(w_ap_k, h_t, cols, reads, evac):
        b = next_pbank()
        for k in range(KD):
            P.op("pe", lambda e, b=b, k=k: e.matmul(banks[b][:, :], w_ap_k(k), h_t[:, k, cols[0]:cols[1]],
                                                   start=(k == 0), stop=(k == KD - 1), skip_group_check=True),
                 reads=reads, writes=[bankb[b]], mark=(k == KD - 1))
        evac(b)

    def phase_swa(l, j):
        m0 = A.mark()
        NT = T // 512
        wkd = [A.alloc([128, KD, 128], BF16) for _ in range(2)]
        wv = [A.alloc([128, KD, 64], BF16) for _ in range(2)]
        wq = [A.alloc([128, KD, 512], BF16) for _ in range(2)]
        wgt = [A.alloc([128, KD, 512], BF16) for _ in range(2)]
        wb = [Buf() for _ in range(2)]
        hts = [A.alloc([128, KD, 512], BF16) for _ in range(3)]
        htsb = [Buf() for _ in range(3)]
        kT = [A.alloc([128, 640], BF16) for _ in range(2)]
        kTb = [Buf() for _ in range(2)]
        vx = [A.alloc([128, 5, 192], BF16) for _ in range(2)]
        vxb = [Buf() for _ in range(2)]
        qs2 = [A.alloc([128, 2, 512], BF16) for _ in range(2)]
        qsA = [t[:, 0, :] for t in qs2]
        qsB = [t[:, 1, :] for t in qs2]
        qsb = [Buf() for _ in range(2)]
        for m_ in range(2):
            P.op("pool", lambda e, m_=m_: e.memset(qs2[m_][:, :, :], 0.0), writes=[qsb[m_]])
        sgs = [A.alloc([128, 4, 512], BF16) for _ in range(2)]
        sgsb = [Buf() for _ in range(2)]
        UA = [A.alloc([128, 512], F32) for _ in range(2)]
        UB = [A.alloc([128, 512], F32) for _ in range(2)]
        Ub = [Buf() for _ in range(2)]
        rec = A.alloc([128, 512], F32)
        recb = Buf()
        zs = [A.alloc([128, 512], BF16) for _ in range(2)]
        zsb = [Buf() for _ in range(2)]
        E = [A.alloc([128, 512], BF16) for _ in range(4)]
        Eb = [Buf() for _ in range(4)]
        pipe = AttnPipe(E, Eb, mask_b)
        for s in range(2):
            P.op("pool", lambda e, s=s: e.memset(vx[s][:, :, :], 1.0), writes=[vxb[s]])

        def load_w(hk):
            s = hk % 2
            kc = DB + hk * 64
            vc = DB + 256 + hk * 64
            for dup in range(2):
                P.dma("pool", wkd[s][:, :, dup * 64:(dup + 1) * 64],
                      b_w_in[0, :, kc:kc + 64].rearrange("(k p) e -> p k e", p=128), writes=[wb[s]])
            P.dma("pool", wv[s][:, :, :], b_w_in[0, :, vc:vc + 64].rearrange("(k p) e -> p k e", p=128),
                  writes=[wb[s]])
            P.dma("pool", wq[s][:, :, :], b_w_in[0, :, hk * 512:(hk + 1) * 512].rearrange("(k p) e -> p k e", p=128),
                  writes=[wb[s]])
            gc = DB + 512 + hk * 512
            P.dma("pool", wgt[s][:, :, :], b_w_in[0, :, gc:gc + 512].rearrange("(k p) e -> p k e", p=128),
                  writes=[wb[s]])

        tiles = [(hk, tt) for hk in range(4) for tt in range(NT)]
        jobs = [(ti, cq) for ti in range(len(tiles)) for cq in range(4)]
        kT3 = kT + [A.alloc([128, 640], BF16)]
        kTb3 = kTb + [Buf()]
        vx3 = vx + [A.alloc([128, 5, 192], BF16)]
        vxb3 = vxb + [Buf()]
        P.op("pool", lambda e: e.memset(vx3[2][:, :, :], 1.0), writes=[vxb3[2]])

        def prologue(ti):
            hk, tt = tiles[ti]
            s = hk % 2
            if tt == 0 and hk + 1 < 4:
                load_w(hk + 1)
            h_t, h_b = hts[ti % 3], htsb[ti % 3]
            k_t, k_b = kT3[ti % 3], kTb3[ti % 3]
            v_t, v_b = vx3[ti % 3], vxb3[ti % 3]
            kn_t, kn_b = kT3[(ti + 1) % 3], kTb3[(ti + 1) % 3]
            vn_t, vn_b = vx3[(ti + 1) % 3], vxb3[(ti + 1) % 3]
            if ti + 1 < len(tiles):
                load_h(ti + 1)
            proj_fm(lambda k, s=s: wkd[s][:, k, :], h_t, (0, 512), [wb[s], h_b],
                    lambda b, k_t=k_t, k_b=k_b: P.op(
                        "act", lambda e: e.activation(out=k_t[:, 128:640], in_=banks[b][:, :], func=AF.Copy),
                        reads=[bankb[b]], writes=[k_b]))
            b = next_pbank()
            for blk in range(4):
                for k in range(KD):
                    P.op("pe", lambda e, b=b, blk=blk, k=k, h_t=h_t, s=s: e.matmul(
                        banks[b][:, blk * 64:(blk + 1) * 64], h_t[:, k, blk * 128:(blk + 1) * 128],
                        wv[s][:, k, :], start=(k == 0), stop=(k == KD - 1), skip_group_check=True),
                        reads=[wb[s], h_b], writes=[bankb[b]], mark=(k == KD - 1))
            P.op("dve", lambda e, b=b, v_t=v_t: e.tensor_copy(
                out=v_t[:, 1:5, 64:128], in_=banks[b][:, 0:256].rearrange("p (b e) -> p b e", b=4)),
                reads=[bankb[b]], writes=[v_b])
            if tt + 1 < NT:
                P.op("pool", lambda e, k_t=k_t, kn_t=kn_t: e.tensor_copy(out=kn_t[:, 0:128], in_=k_t[:, 512:640]),
                     reads=[k_b], writes=[kn_b])
                P.op("pool", lambda e, v_t=v_t, vn_t=vn_t: e.tensor_copy(out=vn_t[:, 0, 64:128],
                                                                      in_=v_t[:, 4, 64:128]),
                     reads=[v_b], writes=[vn_b])
            g4_t, g_b = sgs[ti % 2], sgsb[ti % 2]
            for cq in range(4):
                proj_fm(lambda k, s=s, cq=cq: wgt[s][:, k, cq * 128:(cq + 1) * 128], h_t, (0, 512), [wb[s], h_b],
                        lambda b, cq=cq, g4_t=g4_t, g_b=g_b: P.op(
                            "act", lambda e: e.activation(out=g4_t[:, cq, :], in_=banks[b][:, :], func=AF.Silu),
                            reads=[bankb[b]], writes=[g_b]))

        def qproj(jn):
            ti, cq = jobs[jn]
            hk, tt = tiles[ti]
            s = hk % 2
            h_t, h_b = hts[ti % 3], htsb[ti % 3]
            qa_t, qb_t, q_b = qsA[jn % 2], qsB[jn % 2], qsb[jn % 2]
            proj_fm(lambda k, s=s, cq=cq: wq[s][:, k, cq * 128:(cq + 1) * 128], h_t, (0, 512), [wb[s], h_b],
                    lambda b, qa_t=qa_t, qb_t=qb_t, q_b=q_b: (
                        P.op("dve", lambda e: e.tensor_copy(out=qa_t[0:64, :], in_=banks[b][0:64, :]),
                             reads=[bankb[b]], writes=[q_b]),
                        P.op("dve", lambda e: e.tensor_copy(out=qb_t[64:128, :], in_=banks[b][64:128, :]),
                             reads=[bankb[b]], writes=[q_b])))

        def make_tail(jn):
            ti, cq = jobs[jn]
            hk, tt = tiles[ti]
            c = hk * 4 + cq
            m = jn % 2
            g_t, g_b = sgs[ti % 2][:, cq, :], sgsb[ti % 2]

            def do_tail():
                oz, dz, u_b = UA[m], UB[m], Ub[m]
                P.op("act", lambda e: e.activation(out=rec[:, :], in_=dz[:, :], func=AF.Ln,
                                                   bias=esk2[:, c:c + 1]), reads=[u_b, eskb], writes=[recb])
                P.op("act", lambda e: e.activation(out=rec[:, :], in_=rec[:, :], func=AF.Exp, scale=-1.0),
                     writes=[recb])
                P.op("pool", lambda e: e.tensor_tensor(out=rec[:, :], in0=oz[:, :], in1=rec[:, :], op=ALU.mult),
                     reads=[u_b], writes=[recb])
                P.op("pool", lambda e: e.tensor_tensor(out=zs[m][:, :], in0=rec[:, :], in1=g_t, op=ALU.mult),
                     reads=[recb, g_b], writes=[zsb[m]])
                P.dma("pool", zT[:, c, tt * 512:(tt + 1) * 512], zs[m][:, :], reads=[zsb[m]])
            return do_tail

        def attention(jn, prev_tail):
            ti, cq = jobs[jn]
            hk, tt = tiles[ti]
            m = jn % 2
            k_t, k_b = kT3[ti % 3], kTb3[ti % 3]
            v_t, v_b = vx3[ti % 3], vxb3[ti % 3]
            qa_t, qb_t, q_b = qsA[m], qsB[m], qsb[m]
            ua, ub_, u_b = UA[m], UB[m], Ub[m]
            for blk in range(4):
                has_prev = not (tt == 0 and blk == 0)
                units = []
                for hh in range(2):
                    vsl = (64, 192) if hh == 0 else (0, 128)
                    units.append(dict(
                        q2=qs2[m][:, :, blk * 128:(blk + 1) * 128],
                        kprev=k_t[:, blk * 128:(blk + 1) * 128] if has_prev else None,
                        kcur=k_t[:, (blk + 1) * 128:(blk + 2) * 128],
                        vprev=v_t[:, blk, vsl[0]:vsl[1]] if has_prev else None,
                        vcur=v_t[:, blk + 1, vsl[0]:vsl[1]]))
                cb = None
                if blk % 2 == 1:
                    def cb(oi, blk=blk, ua=ua, ub_=ub_, u_b=u_b):
                        c0 = (blk - 1) * 128
                        ov = banks[oi][:, :].rearrange("p (b h q) -> p b h q", b=2, h=2)
                        for (dst, r0, src_r0, hh) in ((ua, 0, 0, 0), (ua, 64, 64, 1), (ub_, 0, 64, 0), (ub_, 64, 0, 1)):
                            P.op("dve", lambda e, dst=dst, r0=r0, src_r0=src_r0, hh=hh: e.tensor_copy(
                                out=dst[r0:r0 + 64, c0:c0 + 256].rearrange("p (b q) -> p b q", b=2),
                                in_=ov[src_r0:src_r0 + 64, :, hh, :]),
                                reads=[bankb[oi]], writes=[u_b])
                pipe.push(units, [k_b, q_b], [v_b], cb)
            if prev_tail is not None:
                prev_tail()

        def load_h(ti):
            tt = tiles[ti][1]
            P.dma("sp", hts[ti % 3][:, :, :], hT[:, :, tt * 512:(tt + 1) * 512], writes=[htsb[ti % 3]])

        load_w(0)
        load_h(0)
        prologue(0)
        qproj(0)
        prev_tail = None
        for jn in range(len(jobs)):
            if jn + 1 < len(jobs):
                if jobs[jn + 1][1] == 0:
                    prologue(jobs[jn + 1][0])
                qproj(jn + 1)
            attention(jn, prev_tail)
            prev_tail = make_tail(jn)
        pipe.flush()
        prev_tail()
        P.barrier()
        A.release(m0)

    def phase_dil(l, j):
        m0 = A.mark()
        NSB = T // 2048
        DIL = (1, 4, 16)
        wq = [A.alloc([128, KD, 128], BF16) for _ in range(3)]
        wk = [A.alloc([128, KD, 128], BF16) for _ in range(3)]
        wv = [A.alloc([128, KD, 128], BF16) for _ in range(3)]
        wgb = [Buf() for _ in range(3)]
        wgt = A.alloc([128, KD, 128], BF16)
        wgtb = Buf()
        hts = [A.alloc([128, KD, 2048], BF16) for _ in range(2)]
        htsb = [Buf() for _ in range(2)]
        kT = [A.alloc([128, 2, 2048], BF16) for _ in range(3)]
        kTb = [[Buf() for _ in range(2)] for _ in range(3)]
        vx = [A.alloc([128, 2, 16, 192], BF16) for _ in range(3)]
        vxb = [[Buf() for _ in range(2)] for _ in range(3)]
        qT2 = [A.alloc([128, 2, 2048], BF16) for _ in range(2)]
        qTA = [t[:, 0, :] for t in qT2]
        qTB = [t[:, 1, :] for t in qT2]
        qTb = [Buf() for _ in range(2)]
        for m_ in range(2):
            P.op("pool", lambda e, m_=m_: e.memset(qT2[m_][:, :, :], 0.0), writes=[qTb[m_]])
        sgs_ = [A.alloc([128, 2048], BF16) for _ in range(2)]
        sgbs_ = [Buf() for _ in range(2)]
        UA = A.alloc([128, 2048], F32)
        UB = A.alloc([128, 2048], F32)
        Ub = Buf()
        rec = A.alloc([128, 2048], F32)
        recb = Buf()
        zs = A.alloc([128, 2048], BF16)
        zsb = Buf()
        E = [A.alloc([128, 512], BF16) for _ in range(4)]
        Eb = [Buf() for _ in range(4)]
        pipe = AttnPipe(E, Eb, mask_c)
        for g in range(3):
            P.op("pool", lambda e, g=g: e.memset(vx[g][:, :, :, 64:128], 1.0), writes=[vxb[g][0], vxb[g][1]])

        def wcols(idx, c):
            return c_w_in[0, :, idx * DB + c * 128:idx * DB + (c + 1) * 128].rearrange("(k p) e -> p k e", p=128)

        def load_wg(c, g):
            P.dma("pool", wq[g][:, :, :], wcols(3 * g + 0, c), writes=[wgb[g]])
            P.dma("pool", wk[g][:, :, :], wcols(3 * g + 1, c), writes=[wgb[g]])
            P.dma("pool", wv[g][:, :, :], wcols(3 * g + 2, c), writes=[wgb[g]])

        def load_wgate(c):
            P.dma("pool", wgt[:, :, :], wcols(9, c), writes=[wgtb])

        load_wgate(0)
        for g in range(3):
            load_wg(0, g)
        n = 0
        qn = 0
        tailc = [None]
        for c in range(NCH):
            for sb in range(NSB):
                if tailc[0] is not None:
                    tailc[0]()
                    tailc[0] = None
                h_t, h_b = hts[n % 2], htsb[n % 2]
                sg, sgb = sgs_[n % 2], sgbs_[n % 2]
                n += 1
                half = sb % 2
                for q4 in range(4):
                    P.dma("sp", h_t[:, :, q4 * 512:(q4 + 1) * 512],
                          hT[:, :, sb * 2048 + q4 * 512:sb * 2048 + (q4 + 1) * 512], writes=[h_b])
                for tl in range(4):
                    proj_fm(lambda k: wgt[:, k, :], h_t, (tl * 512, (tl + 1) * 512), [wgtb, h_b],
                            lambda b, tl=tl, sg=sg, sgb=sgb: P.op(
                                "act", lambda e: e.activation(out=sg[:, tl * 512:(tl + 1) * 512], in_=banks[b][:, :],
                                                              func=AF.Silu),
                                reads=[bankb[b]], writes=[sgb]))
                if sb == NSB - 1 and c + 1 < NCH:
                    load_wgate(c + 1)
                for g in range(3):
                    d = DIL[g]
                    nI = 16 // d
                    qa_t, qb_t, q_b = qTA[qn % 2], qTB[qn % 2], qTb[qn % 2]
                    q2_t = qT2[qn % 2]
                    qn += 1
                    k_t = kT[g]
                    v_t = vx[g]
                    for tl in range(4):
                        proj_fm(lambda k, g=g: wk[g][:, k, :], h_t, (tl * 512, (tl + 1) * 512), [wgb[g], h_b],
                                lambda b, tl=tl, k_t=k_t, g=g, half=half: P.op(
                                    "act", lambda e: e.activation(out=k_t[:, half, tl * 512:(tl + 1) * 512],
                                                                  in_=banks[b][:, :], func=AF.Copy),
                                    reads=[bankb[b]], writes=[kTb[g][half]]))
                    for tl in range(4):
                        proj_fm(lambda k, g=g: wq[g][:, k, :], h_t, (tl * 512, (tl + 1) * 512), [wgb[g], h_b],
                                lambda b, tl=tl, qa_t=qa_t, qb_t=qb_t, q_b=q_b: (
                                    P.op("dve", lambda e: e.tensor_copy(out=qa_t[0:64, tl * 512:(tl + 1) * 512],
                                                                        in_=banks[b][0:64, :]),
                                         reads=[bankb[b]], writes=[q_b]),
                                    P.op("dve", lambda e: e.tensor_copy(out=qb_t[64:128, tl * 512:(tl + 1) * 512],
                                                                        in_=banks[b][64:128, :]),
                                         reads=[bankb[b]], writes=[q_b])))
                    def blk_start(bk):
                        i, r = bk // d, bk % d
                        return i * 128 * d + r
                    for b4 in range(4):
                        b = next_pbank()
                        for bb in range(4):
                            bk = b4 * 4 + bb
                            st = blk_start(bk)
                            for k in range(KD):
                                P.op("pe", lambda e, b=b, bb=bb, st=st, k=k, g=g, d=d, h_t=h_t: e.matmul(
                                    banks[b][:, bb * 128:(bb + 1) * 128], h_t[:, k, st:st + 127 * d + 1:d],
                                    wv[g][:, k, :], start=(k == 0), stop=(k == KD - 1), skip_group_check=True),
                                    reads=[wgb[g], h_b], writes=[bankb[b]], mark=(k == KD - 1))
                        P.op("dve", lambda e, b=b, b4=b4, v_t=v_t, half=half: e.tensor_copy(
                            out=v_t[:, half, b4 * 4:(b4 + 1) * 4, :].rearrange("p b (t e) -> p b t e", t=3)[:, :, 0:3:2, :],
                            in_=banks[b][:, :].rearrange("p (b t e) -> p b t e", b=4, t=2)),
                            reads=[bankb[b]], writes=[vxb[g][half]])
                    if sb == NSB - 1 and c + 1 < NCH:
                        load_wg(c + 1, g)
                    for bk in range(16):
                        i, r = bk // d, bk % d
                        st = blk_start(bk)
                        if i >= 1:
                            ph, pbk = half, (i - 1) * d + r
                            has_prev = True
                        else:
                            ph, pbk = 1 - half, (nI - 1) * d + r
                            has_prev = sb >= 1
                        pst = blk_start(pbk)
                        units = []
                        for hh in range(2):
                            r0, r1 = hh * 64, hh * 64 + 64
                            vsl = (0, 128) if hh == 0 else (64, 192)
                            units.append(dict(
                                q2=q2_t[:, :, st:st + 127 * d + 1:d],
                                kprev=k_t[:, ph, pst:pst + 127 * d + 1:d] if has_prev else None,
                                kcur=k_t[:, half, st:st + 127 * d + 1:d],
                                vprev=v_t[:, ph, pbk, vsl[0]:vsl[1]] if has_prev else None,
                                vcur=v_t[:, half, bk, vsl[0]:vsl[1]]))
                        cb = None
                        if bk % 2 == 1:
                            def cb(oi, bk=bk, g=g, d=d, st0=blk_start(bk - 1)):
                                bstr = 1 if d > 1 else 128
                                ov = banks[oi][:, :].rearrange("p (b h q) -> p b h q", b=2, h=2)
                                for hh, U in ((0, UA), (1, UB)):
                                    uv = bass.AP(U, st0, [[2048, 128], [bstr, 2], [d, 128]])
                                    if g == 0:
                                        P.op("dve", lambda e, uv=uv, hh=hh: e.tensor_copy(out=uv, in_=ov[:, :, hh, :]),
                                             reads=[bankb[oi]], writes=[Ub])
                                    else:
                                        P.op("dve", lambda e, uv=uv, hh=hh: e.tensor_tensor(
                                            out=uv, in0=ov[:, :, hh, :], in1=uv, op=ALU.add),
                                            reads=[bankb[oi]], writes=[Ub])
                        kr = [kTb[g][half], q_b] + ([kTb[g][ph]] if has_prev and ph != half else [])
                        vr = [vxb[g][half]] + ([vxb[g][ph]] if has_prev and ph != half else [])
                        pipe.push(units, kr, vr, cb)

                def do_tail(sg=sg, sgb=sgb, c=c, sb=sb):
                    pipe.flush()
                    finalize(UA, UB, Ub, rec, recb, sg, sgb, zs, zsb, 2048, None)
                    P.dma("pool", zT[:, c, sb * 2048:(sb + 1) * 2048], zs[:, :], reads=[zsb])
                tailc[0] = do_tail
        tailc[0]()
        P.barrier()
        A.release(m0)

    def phase_norm(x_src, l):
        m0 = A.mark()
        xt = [A.alloc([128, DM], F32) for _ in range(3)]
        xtb = [Buf() for _ in range(3)]
        sq = A.alloc([128, DM], F32)
        sqb = Buf()
        ss = [A.alloc([128, 2], F32) for _ in range(2)]
        ssb = [Buf() for _ in range(2)]
        hb = [A.alloc([128, DM], BF16) for _ in range(2)]
        hbb = [Buf() for _ in range(2)]
        hts = [A.alloc([128, KD, 512], BF16) for _ in range(2)]
        htsb = [Buf() for _ in range(2)]
        for i in range(T // 128):
            x_t, x_b = xt[i % 3], xtb[i % 3]
            s_t, s_b = ss[i % 2], ssb[i % 2]
            h_t, h_b = hb[i % 2], hbb[i % 2]
            o_t, o_b = hts[(i // 4) % 2], htsb[(i // 4) % 2]
            bk, bkb = banks[i % 2], bankb[i % 2]
            P.dma("sp", x_t[:, :], x_src[i * 128:(i + 1) * 128, :], writes=[x_b])
            P.op("act", lambda e, x_t=x_t: e.activation(out=sq[:, :], in_=x_t[:, :], func=AF.Square),
                 reads=[x_b], writes=[sqb])
            P.op("dve", lambda e, s_t=s_t: e.reduce_sum(out=s_t[:, 0:1], in_=sq[:, :], axis=AX.X),
                 reads=[sqb], writes=[s_b])
            P.op("act", lambda e, s_t=s_t: e.activation(out=s_t[:, 1:2], in_=s_t[:, 0:1], func=AF.Sqrt,
                                                         bias=epst[:, 0:1], scale=1.0 / DM),
                 reads=[epsb], writes=[s_b])
            P.op("dve", lambda e, s_t=s_t: e.reciprocal(out=s_t[:, 0:1], in_=s_t[:, 1:2]), writes=[s_b])
            P.op("act", lambda e, x_t=x_t, s_t=s_t, h_t=h_t: e.activation(
                out=h_t[:, :], in_=x_t[:, :], func=AF.Copy, scale=s_t[:, 0:1]),
                reads=[x_b, s_b], writes=[h_b])
            pT = bk[:, :].bitcast(BF16)
            for c in range(KD):
                P.op("pe", lambda e, c=c, pT=pT, h_t=h_t: e.transpose(
                    out=pT[:, c * 128:(c + 1) * 128], in_=h_t[:, c * 128:(c + 1) * 128], identity=ident[:, :]),
                    reads=[h_b, identb], writes=[bkb], mark=(c == KD - 1))
            sl = (i % 4) * 128
            P.op("dve", lambda e, pT=pT, o_t=o_t, sl=sl: e.tensor_tensor(
                out=o_t[:, :, sl:sl + 128], in0=pT.rearrange("p (c t) -> p c t", c=KD),
                in1=gT[:, l, :].unsqueeze(2).to_broadcast([128, KD, 128]), op=ALU.mult),
                reads=[bkb, gTb], writes=[o_b])
            if i % 4 == 3:
                tt = i // 4
                P.dma("pool", hT[:, :, tt * 512:(tt + 1) * 512], o_t[:, :, :], reads=[o_b])
        P.barrier()
        A.release(m0)

    def phase_out(x_src, x_dst, l, last, next_l=None):
        m0 = A.mark()
        if next_l is not None:
            nsq = A.alloc([128, DM], F32)
            nsqb = Buf()
            nss = [A.alloc([128, 2], F32) for _ in range(3)]
            nssb = [Buf() for _ in range(3)]
            nhb = [A.alloc([128, DM], BF16) for _ in range(3)]
            nhbb = [Buf() for _ in range(3)]
            nhts = [A.alloc([128, KD, 512], BF16) for _ in range(2)]
            nhtsb = [Buf() for _ in range(2)]
        wo = A.alloc([128, NCH, DM], BF16)
        wobs = [Buf() for _ in range(NCH)]
        for c in range(NCH):
            P.dma("pool", wo[:, c, :], w_out[l, c * 128:(c + 1) * 128, :], writes=[wobs[c]])
        zt = [A.alloc([128, NCH, 512], BF16) for _ in range(2)]
        ztb = [Buf() for _ in range(2)]
        xt = [A.alloc([128, DM], F32) for _ in range(3)]
        xtb = [Buf() for _ in range(3)]
        xo = [A.alloc([128, DM], F32) for _ in range(2)]
        xob = [Buf() for _ in range(2)]
        if last:
            gf = A.alloc([128, DM], F32)
            gfb = Buf()
            P.dma("sp", gf[:, :], final_g[0:1, :].partition_broadcast(128), writes=[gfb], slow=True)
            sq = A.alloc([128, DM], F32)
            sqb = Buf()
            ss = [A.alloc([128, 2], F32) for _ in range(2)]
            ssb = [Buf() for _ in range(2)]
            yo = [A.alloc([128, DM], F32) for _ in range(2)]
            yob = [Buf() for _ in range(2)]
        pend_tr = []
        for i in range(T // 128):
            tt, s = i // 4, i % 4
            z_t, z_b = zt[tt % 2], ztb[tt % 2]
            if s == 0:
                P.dma("sp", z_t[:, :, :], zT[:, :, tt * 512:(tt + 1) * 512], writes=[z_b])
            x_t, x_b = xt[i % 3], xtb[i % 3]
            o_t, o_b = xo[i % 2], xob[i % 2]
            P.dma("sp", x_t[:, :], x_src[i * 128:(i + 1) * 128, :], writes=[x_b])
            b0 = 4 + 2 * (i % 2)
            for c in range(NCH):
                for half in range(2):
                    P.op("pe", lambda e, c=c, half=half, b0=b0, z_t=z_t, s=s: e.matmul(
                        banks[b0 + half][:, :], z_t[:, c, s * 128:(s + 1) * 128],
                        wo[:, c, half * 512:(half + 1) * 512], start=(c == 0), stop=(c == NCH - 1),
                        skip_group_check=True),
                        reads=[z_b, wobs[c]], writes=[bankb[b0 + half]], mark=(c == NCH - 1))
            if len(pend_tr) >= 2:
                pend_tr.pop(0)()
            for half in range(2):
                P.op("dve", lambda e, half=half, b0=b0, x_t=x_t, o_t=o_t: e.tensor_tensor(
                    out=o_t[:, half * 512:(half + 1) * 512], in0=banks[b0 + half][:, :],
                    in1=x_t[:, half * 512:(half + 1) * 512], op=ALU.add),
                    reads=[bankb[b0 + half], x_b], writes=[o_b])
            if not last:
                P.dma("pool", x_dst[i * 128:(i + 1) * 128, :], o_t[:, :], reads=[o_b])
                if next_l is not None:
                    s_t, s_b = nss[i % 3], nssb[i % 3]
                    h_t, h_b = nhb[i % 3], nhbb[i % 3]
                    ho_t, ho_b = nhts[(i // 4) % 2], nhtsb[(i // 4) % 2]
                    bk, bkb = banks[i % 2], bankb[i % 2]
                    P.op("act", lambda e, o_t=o_t: e.activation(out=nsq[:, :], in_=o_t[:, :], func=AF.Square),
                         reads=[o_b], writes=[nsqb])
                    P.op("dve", lambda e, s_t=s_t: e.reduce_sum(out=s_t[:, 0:1], in_=nsq[:, :], axis=AX.X),
                         reads=[nsqb], writes=[s_b])
                    P.op("act", lambda e, s_t=s_t: e.activation(out=s_t[:, 1:2], in_=s_t[:, 0:1], func=AF.Sqrt,
                                                                 bias=epst[:, 0:1], scale=1.0 / DM),
                         reads=[epsb], writes=[s_b])
                    P.op("dve", lambda e, s_t=s_t: e.reciprocal(out=s_t[:, 0:1], in_=s_t[:, 1:2]), writes=[s_b])
                    P.op("act", lambda e, o_t=o_t, s_t=s_t, h_t=h_t: e.activation(
                        out=h_t[:, :], in_=o_t[:, :], func=AF.Copy, scale=s_t[:, 0:1]),
                        reads=[o_b, s_b], writes=[h_b])
                    def do_tr(i=i, h_t=h_t, h_b=h_b, ho_t=ho_t, ho_b=ho_b, bk=bk, bkb=bkb):
                        pT = bk[:, :].bitcast(BF16)
                        for c in range(KD):
                            P.op("pe", lambda e, c=c, pT=pT, h_t=h_t: e.transpose(
                                out=pT[:, c * 128:(c + 1) * 128], in_=h_t[:, c * 128:(c + 1) * 128],
                                identity=ident[:, :]),
                                reads=[h_b, identb], writes=[bkb], mark=(c == KD - 1))
                        sl = (i % 4) * 128
                        P.op("dve", lambda e, pT=pT, ho_t=ho_t, sl=sl: e.tensor_tensor(
                            out=ho_t[:, :, sl:sl + 128], in0=pT.rearrange("p (c t) -> p c t", c=KD),
                            in1=gT[:, next_l, :].unsqueeze(2).to_broadcast([128, KD, 128]), op=ALU.mult),
                            reads=[bkb, gTb], writes=[ho_b])
                        if i % 4 == 3:
                            P.dma("pool", hT[:, :, (i // 4) * 512:(i // 4 + 1) * 512], ho_t[:, :, :], reads=[ho_b])
                    pend_tr.append(do_tr)
            else:
                s_t, s_b = ss[i % 2], ssb[i % 2]
                y_t, y_b = yo[i % 2], yob[i % 2]
                P.op("act", lambda e, o_t=o_t: e.activation(out=sq[:, :], in_=o_t[:, :], func=AF.Square),
                     reads=[o_b], writes=[sqb])
                P.op("dve", lambda e, s_t=s_t: e.reduce_sum(out=s_t[:, 0:1], in_=sq[:, :], axis=AX.X),
                     reads=[sqb], writes=[s_b])
                P.op("act", lambda e, s_t=s_t: e.activation(out=s_t[:, 1:2], in_=s_t[:, 0:1], func=AF.Sqrt,
                                                             bias=epst[:, 0:1], scale=1.0 / DM),
                     reads=[epsb], writes=[s_b])
                P.op("dve", lambda e, s_t=s_t: e.reciprocal(out=s_t[:, 0:1], in_=s_t[:, 1:2]), writes=[s_b])
                P.op("dve", lambda e, s_t=s_t, o_t=o_t, y_t=y_t: e.scalar_tensor_tensor(
                    out=y_t[:, :], in0=o_t[:, :], scalar=s_t[:, 0:1], in1=gf[:, :], op0=ALU.mult, op1=ALU.mult),
                    reads=[o_b, s_b, gfb], writes=[y_b])
                P.dma("pool", x_dst[i * 128:(i + 1) * 128, :], y_t[:, :], reads=[y_b])
        while pend_tr:
            pend_tr.pop(0)()
        P.barrier()
        A.release(m0)

    def phase_pool(l, j):
        m0 = A.mark()
        wu = [A.alloc([128, KD, 512], BF16) for _ in range(2)]
        wg = [A.alloc([128, KD, 512], BF16) for _ in range(2)]
        wp = [A.alloc([128, 4, 512], BF16) for _ in range(2)]
        wb = [Buf() for _ in range(2)]
        hts = [A.alloc([128, KD, 512], BF16) for _ in range(3)]
        htsb = [Buf() for _ in range(3)]
        ub = [A.alloc([128, 4, 528], F32) for _ in range(2)]
        ubb = [Buf() for _ in range(2)]
        ta = A.alloc([128, 4, 528], F32)
        tb = A.alloc([128, 4, 528], F32)
        tab, tbb = Buf(), Buf()
        dbs = [A.alloc([128, 4, 512], BF16) for _ in range(2)]
        dbbs = [Buf() for _ in range(2)]
        t16 = A.alloc([128, 4, 16], F32)
        t16b = Buf()
        sg = [A.alloc([128, 2, 512], BF16) for _ in range(2)]
        sgb = [Buf() for _ in range(2)]
        zs = [A.alloc([128, 4, 512], BF16) for _ in range(2)]
        zsb = [Buf() for _ in range(2)]
        NT = T // 512
        steps = [(gi, tt) for gi in range(4) for tt in range(NT)]

        def load_w(gi):
            s = gi % 2
            P.dma("pool", wu[s][:, :, :], a_w_in[j, :, gi * 512:(gi + 1) * 512].rearrange("(k p) e -> p k e", p=128),
                  writes=[wb[s]])
            P.dma("pool", wg[s][:, :, :],
                  a_w_in[j, :, DB + gi * 512:DB + (gi + 1) * 512].rearrange("(k p) e -> p k e", p=128),
                  writes=[wb[s]])
            P.dma("pool", wp[s][:, :, :], a_w_group[j, gi, :, :].rearrange("(k p) e -> p k e", p=128),
                  writes=[wb[s]])

        def emit_u(n):
            gi, tt = steps[n]
            s = gi % 2
            h_t, h_b = hts[n % 3], htsb[n % 3]
            u_t, u_b = ub[n % 2], ubb[n % 2]
            for c in range(4):
                for k in range(KD):
                    P.op("pe", lambda e, c=c, k=k, s=s, h_t=h_t: e.matmul(
                        banks[c][:, :], wu[s][:, k, c * 128:(c + 1) * 128], h_t[:, k, :],
                        start=(k == 0), stop=(k == KD - 1), skip_group_check=True),
                        reads=[wb[s], h_b], writes=[bankb[c]], mark=(k == KD - 1))
            if tt == 0:
                P.op("pool", lambda e, u_t=u_t: e.memset(u_t[:, :, 0:16], 0.0), writes=[u_b])
            else:
                up_t, up_b = ub[(n - 1) % 2], ubb[(n - 1) % 2]
                P.op("pool", lambda e, u_t=u_t, up_t=up_t: e.tensor_copy(out=u_t[:, :, 0:16], in_=up_t[:, :, 512:528]),
                     reads=[up_b], writes=[u_b])
            for c in range(4):
                P.op("act", lambda e, c=c, u_t=u_t: e.activation(out=u_t[:, c, 16:528], in_=banks[c][:, :],
                                                                func=AF.Copy),
                     reads=[bankb[c]], writes=[u_b])

        def emit_pool(n):
            gi, tt = steps[n]
            u_t, u_b = ub[n % 2], ubb[n % 2]
            db, dbb = dbs[n % 2], dbbs[n % 2]
            src, srcb = u_t, u_b
            lo = 0
            tmps = [(ta, tab), (tb, tbb)]
            for lvl in range(gi + 1):
                sh = 1 << lvl
                lo = lo + sh
                dst, dstb = tmps[lvl % 2]
                P.op("dve", lambda e, src=src, dst=dst, lo=lo, sh=sh: e.tensor_tensor(
                    out=dst[:, :, lo:528], in0=src[:, :, lo:528], in1=src[:, :, lo - sh:528 - sh], op=ALU.add),
                    reads=[srcb], writes=[dstb])
                src, srcb = dst, dstb
            w = 2 << gi
            P.op("dve", lambda e, src=src, u_t=u_t, w=w: e.scalar_tensor_tensor(
                out=db[:, :, :], in0=src[:, :, 16:528], scalar=1.0 / w, in1=u_t[:, :, 16:528],
                op0=ALU.mult, op1=ALU.subtract), reads=[srcb, u_b], writes=[dbb])
            if tt == 0:
                P.op("dve", lambda e, src=src, gi=gi: e.tensor_tensor(
                    out=t16[:, :, :], in0=src[:, :, 16:32],
                    in1=rcw[:, gi, :].unsqueeze(1).to_broadcast([128, 4, 16]), op=ALU.mult),
                    reads=[srcb, rcb], writes=[t16b])
                P.op("dve", lambda e, u_t=u_t: e.tensor_tensor(
                    out=db[:, :, 0:16], in0=t16[:, :, :], in1=u_t[:, :, 16:32], op=ALU.subtract),
                    reads=[t16b, u_b], writes=[dbb])

        def emit_rest(n):
            gi, tt = steps[n]
            s = gi % 2
            h_t, h_b = hts[n % 3], htsb[n % 3]
            z_t, z_b = zs[n % 2], zsb[n % 2]
            db, dbb = dbs[n % 2], dbbs[n % 2]
            for half in range(2):
                g_t, g_b = sg[half], sgb[half]
                for cc in range(2):
                    co = half * 2 + cc
                    for k in range(4):
                        P.op("pe", lambda e, co=co, k=k, s=s, cc=cc: e.matmul(
                            banks[4 + cc][:, :], wp[s][:, k, co * 128:(co + 1) * 128], db[:, k, :],
                            start=(k == 0), stop=(k == 3), skip_group_check=True),
                            reads=[wb[s], dbb], writes=[bankb[4 + cc]], mark=(k == 3))
                for cc in range(2):
                    co = half * 2 + cc
                    for k in range(KD):
                        P.op("pe", lambda e, co=co, k=k, s=s, cc=cc, h_t=h_t: e.matmul(
                            banks[6 + cc][:, :], wg[s][:, k, co * 128:(co + 1) * 128], h_t[:, k, :],
                            start=(k == 0), stop=(k == KD - 1), skip_group_check=True),
                            reads=[wb[s], h_b], writes=[bankb[6 + cc]], mark=(k == KD - 1))
                for cc in range(2):
                    P.op("act", lambda e, cc=cc, g_t=g_t: e.activation(out=g_t[:, cc, :], in_=banks[6 + cc][:, :],
                                                                    func=AF.Silu),
                         reads=[bankb[6 + cc]], writes=[g_b])
                for cc in range(2):
                    co = half * 2 + cc
                    ch = gi * 4 + co
                    P.op("dve", lambda e, cc=cc, co=co, ch=ch, g_t=g_t, z_t=z_t: e.scalar_tensor_tensor(
                        out=z_t[:, co, :], in0=banks[4 + cc][:, :], scalar=scT[:, j, ch:ch + 1],
                        in1=g_t[:, cc, :], op0=ALU.mult, op1=ALU.mult),
                        reads=[bankb[4 + cc], g_b, scTb], writes=[z_b])
            P.dma("pool", zT[:, gi * 4:(gi + 1) * 4, tt * 512:(tt + 1) * 512], z_t[:, :, :], reads=[z_b])

        def load_h(n):
            tt = steps[n][1]
            P.dma("sp", hts[n % 3][:, :, :], hT[:, :, tt * 512:(tt + 1) * 512], writes=[htsb[n % 3]])

        load_w(0)
        load_h(0)
        load_h(1)
        emit_u(0)
        emit_pool(0)
        for n in range(len(steps)):
            gi, tt = steps[n]
            if tt == 0 and gi + 1 < 4:
                load_w(gi + 1)
            if n + 2 < len(steps):
                load_h(n + 2)
            if n + 1 < len(steps):
                emit_u(n + 1)
                emit_pool(n + 1)
            emit_rest(n)
        P.barrier()
        A.release(m0)

    cur = x_in
    nl = len(layers)
    for li, l in enumerate(layers):
        last = (li == nl - 1) and final_norm
        dst = y_out if li == nl - 1 else xs[li % 2]
        kind, j = l % 3, l // 3
        if li == 0:
            phase_norm(cur, l)
        if kind == 0:
            phase_pool(l, j)
        elif kind == 1:
            phase_swa(l, j)
        else:
            phase_dil(l, j)
        phase_out(cur, dst, l, last, next_l=(layers[li + 1] if li + 1 < nl else None))
        cur = dst

    with nc.Block() as block:
        P.run_all(block)
    es.close()
    return nc


_INPUT_NAMES = ("norm_g", "final_g", "w_out", "a_w_in", "a_w_group", "a_scale", "b_w_in", "b_sinks", "c_w_in")


def _in_maps(inputs, x_per_core):
    maps = []
    shared = {}
    for k in _INPUT_NAMES:
        v = np.ascontiguousarray(np.asarray(inputs[k], dtype=np.float32))
        if k == "final_g":
            v = v.reshape(1, DM)
        shared[k] = v
    for c in range(len(x_per_core)):
        m = dict(shared)
        m["x"] = np.ascontiguousarray(x_per_core[c])
        maps.append(m)
    return maps


def kernel(**inputs):
    x = np.asarray(inputs["x"], dtype=np.float32)
    nc = build_program()
    maps = _in_maps(inputs, [x[b] for b in range(N_CORES)])
    res = run_bass_kernel_spmd(nc, maps, core_ids=list(range(N_CORES)))
    return np.stack([np.asarray(res.results[b]["out"], dtype=np.float32) for b in range(N_CORES)], axis=0)
```

```python
import numpy as np
from contextlib import ExitStack
import concourse.bass as bass
import concourse.mybir as mybir
from concourse.bass_utils import run_bass_kernel_spmd

F32 = mybir.dt.float32
BF16 = mybir.dt.bfloat16
AF = mybir.ActivationFunctionType
ALU = mybir.AluOpType
AX = mybir.AxisListType

T = 8192
DM = 1024
DB = 2048
NCH = 16
KD = 8
EPS = 1e-5
N_CORES = 8


class Buf:
    __slots__ = ("w", "r")

    def __init__(self):
        self.w = {}
        self.r = {}


class Prog:
    STREAMS = ("sp", "act", "pe", "dve", "pool")
    COMPUTE = ("act", "pe", "dve", "pool")
    DMAQ = ("sp", "pool")
    NDMA = 24

    def __init__(self, nc, es):
        self.nc = nc
        self.ops = {s: [] for s in self.STREAMS}
        self.idx = {s: 0 for s in self.STREAMS}
        self.seen = {s: {} for s in self.STREAMS}
        self.done = {}
        self.cnt = {}
        for s in self.COMPUTE:
            self.done[s] = es.enter_context(nc.semaphore("done_" + s))
            self.cnt[s] = 0
        self.dsem = {q: [es.enter_context(nc.semaphore("d%s%d" % (q, i))) for i in range(self.NDMA)]
                     for q in self.DMAQ}
        self.dcnt = {q: [0] * self.NDMA for q in self.DMAQ}
        self.drr = {q: 0 for q in self.DMAQ}

    def _waits(self, stream, reads, writes):
        need = {}
        seen = self.seen[stream]
        cur = self.idx[stream]

        def add(tok):
            key, sem, val, eng, idx = tok
            if eng == stream:
                if stream == "pe":
                    return
                if cur - idx > 2:
                    return
            if seen.get(key, 0) >= val:
                return
            if key not in need or need[key][1] < val:
                need[key] = (sem, val)

        for b in reads:
            for t in b.w.values():
                add(t)
        for b in writes:
            for t in b.w.values():
                add(t)
            for t in b.r.values():
                add(t)
        for key, (sem, val) in need.items():
            seen[key] = val
        return list(need.values())

    def _update(self, tok, reads, writes):
        key = tok[0]
        for b in writes:
            b.w = {key: tok}
            b.r = {}
        for b in reads:
            if b not in writes:
                b.r[key] = tok

    def op(self, stream, fn, reads=(), writes=(), mark=True):
        waits = self._waits(stream, reads, writes)
        i = self.idx[stream]
        self.idx[stream] = i + 1
        if mark:
            self.cnt[stream] += 1
            val = self.cnt[stream]
        else:
            val = self.cnt[stream] + 1
        sem = self.done[stream]
        tok = ("E" + stream, sem, val, stream, i)

        def run(e, waits=waits, fn=fn, mark=mark, sem=sem):
            for (s, v) in waits:
                e.wait_ge(s, v)
            ins = fn(e)
            if mark:
                ins.then_inc(sem, 1)

        self.ops[stream].append(run)
        self._update(tok, reads, writes)
        return tok

    def dma(self, q, out, in_, reads=(), writes=(), slow=False):
        waits = self._waits(q, reads, writes)
        k = self.drr[q]
        self.drr[q] = (k + 1) % self.NDMA
        sem = self.dsem[q][k]
        prev = self.dcnt[q][k]
        key = "D%s%d" % (q, k)
        if prev > 0 and self.seen[q].get(key, 0) < prev:
            waits.append((sem, prev))
            self.seen[q][key] = prev
        self.dcnt[q][k] = prev + 16
        i = self.idx[q]
        self.idx[q] = i + 1
        tok = (key, sem, prev + 16, None, i)

        def run(e, waits=waits, out=out, in_=in_, sem=sem, slow=slow):
            for (s, v) in waits:
                e.wait_ge(s, v)
            if slow:
                e.dma_start(out=out, in_=in_, allow_slow_non_contiguous=True).then_inc(sem, 16)
            else:
                e.dma_start(out=out, in_=in_).then_inc(sem, 16)

        self.ops[q].append(run)
        self._update(tok, reads, writes)
        return tok

    def barrier(self):
        targets = []
        for E in self.COMPUTE:
            if self.cnt[E] > 0:
                targets.append(("E" + E, self.done[E], self.cnt[E]))
        for q in self.DMAQ:
            for k in range(self.NDMA):
                if self.dcnt[q][k] > 0:
                    targets.append(("D%s%d" % (q, k), self.dsem[q][k], self.dcnt[q][k]))
        for s in self.STREAMS:
            waits = []
            for key, sem, val in targets:
                if self.seen[s].get(key, 0) < val:
                    waits.append((sem, val))
                    self.seen[s][key] = val

            def run(e, waits=waits):
                for (sm, v) in waits:
                    e.wait_ge(sm, v)

            self.ops[s].append(run)

    def run_all(self, block):
        ops = self.ops

        @block.sync
        def _(e):
            for f in ops["sp"]:
                f(e)

        @block.scalar
        def _(e):
            for f in ops["act"]:
                f(e)

        @block.tensor
        def _(e):
            for f in ops["pe"]:
                f(e)

        @block.vector
        def _(e):
            for f in ops["dve"]:
                f(e)

        @block.gpsimd
        def _(e):
            for f in ops["pool"]:
                f(e)


class Arena:
    def __init__(self, nc, base, limit):
        self.nc = nc
        self.off = base
        self.limit = limit
        self.n = 0
        self.cache = {}

    def alloc(self, shape, dtype):
        nb = mybir.dt.size(dtype)
        for s in shape[1:]:
            nb *= s
        nb = (nb + 63) // 64 * 64
        key = (self.off, tuple(shape), str(dtype))
        if key not in self.cache:
            self.cache[key] = self.nc.alloc_sbuf_tensor_at("ar%d" % self.n, list(shape), dtype, offset=self.off)
            self.n += 1
        t = self.cache[key]
        self.off += nb
        assert self.off <= self.limit, ("SBUF arena overflow", self.off)
        return t

    def mark(self):
        return self.off

    def release(self, m):
        self.off = m


def build_program(layers=(0, 1, 2, 3), final_norm=True, tokens=8192):
    global T
    T = tokens
    nc = bass.Bass("TRN2", target_bir_lowering=False)
    es = ExitStack()
    x_in = nc.dram_tensor("x", [T, DM], F32, kind="ExternalInput").ap()
    norm_g = nc.dram_tensor("norm_g", [4, DM], F32, kind="ExternalInput").ap()
    final_g = nc.dram_tensor("final_g", [1, DM], F32, kind="ExternalInput").ap()
    w_out = nc.dram_tensor("w_out", [4, DB, DM], F32, kind="ExternalInput").ap()
    a_w_in = nc.dram_tensor("a_w_in", [2, DM, 2 * DB], F32, kind="ExternalInput").ap()
    a_w_group = nc.dram_tensor("a_w_group", [2, 4, 512, 512], F32, kind="ExternalInput").ap()
    a_scale = nc.dram_tensor("a_scale", [2, DB], F32, kind="ExternalInput").ap()
    b_w_in = nc.dram_tensor("b_w_in", [1, DM, 4608], F32, kind="ExternalInput").ap()
    b_sinks = nc.dram_tensor("b_sinks", [1, 32], F32, kind="ExternalInput").ap()
    c_w_in = nc.dram_tensor("c_w_in", [1, DM, 20480], F32, kind="ExternalInput").ap()
    y_out = nc.dram_tensor("out", [T, DM], F32, kind="ExternalOutput").ap()
    xs = [nc.dram_tensor("xs%d" % i, [T, DM], F32, kind="Internal").ap() for i in range(2)]
    hT = nc.dram_tensor("hT", [128, KD, T], BF16, kind="Internal").ap()
    zT = nc.dram_tensor("zT", [128, NCH, T], BF16, kind="Internal").ap()

    P = Prog(nc, es)
    rem = nc.sbuf_bytes_remaining - 4096
    resv = nc.alloc_sbuf_tensor("arena_resv", [128, rem], mybir.dt.uint8)
    abase = nc.lookup_mloc(resv).addr
    A = Arena(nc, abase, abase + rem)
    banks = [nc.alloc_psum_tensor("bank%d" % i, [128, 512], F32) for i in range(8)]
    bankb = [Buf() for _ in range(8)]

    ident = A.alloc([128, 128], BF16)
    identb = Buf()
    tmpf = A.alloc([128, 256], F32)
    tmpfb = Buf()
    mask_b = A.alloc([128, 256], BF16)
    mask_c = A.alloc([128, 256], BF16)
    maskb = Buf()
    gT = A.alloc([128, 4, KD], F32)
    gTb = Buf()
    scT = A.alloc([128, 2, NCH], F32)
    scTb = Buf()
    esk = A.alloc([128, 32], F32)
    eskb = Buf()
    rc16 = A.alloc([128, 16], F32)
    rcw = A.alloc([128, 4, 16], F32)
    rcb = Buf()
    epst = A.alloc([128, 1], F32)
    epsb = Buf()
    P.op("pool", lambda e: e.memset(epst[:, :], EPS), writes=[epsb])

    P.op("pool", lambda e: e.memset(tmpf[:, 0:128], 0.0), writes=[tmpfb])
    P.op("pool", lambda e: e.affine_select(out=tmpf[:, 0:128], in_=tmpf[:, 0:128], pattern=[[-1, 128]],
                                           compare_op=ALU.not_equal, fill=1.0, base=0, channel_multiplier=1),
         writes=[tmpfb])
    P.op("pool", lambda e: e.tensor_copy(out=ident[:, :], in_=tmpf[:, 0:128]), reads=[tmpfb], writes=[identb])
    for (mt, prev_op) in ((mask_b, ALU.is_gt), (mask_c, ALU.is_ge)):
        P.op("pool", lambda e: e.memset(tmpf[:, :], 1.0), reads=[], writes=[tmpfb])
        P.op("pool", lambda e, prev_op=prev_op: e.affine_select(
            out=tmpf[:, 0:128], in_=tmpf[:, 0:128], pattern=[[-1, 128]], compare_op=prev_op, fill=0.0,
            base=0, channel_multiplier=1), writes=[tmpfb])
        P.op("pool", lambda e: e.affine_select(
            out=tmpf[:, 128:256], in_=tmpf[:, 128:256], pattern=[[1, 128]], compare_op=ALU.is_ge, fill=0.0,
            base=0, channel_multiplier=-1), writes=[tmpfb])
        P.op("pool", lambda e, mt=mt: e.tensor_copy(out=mt[:, :], in_=tmpf[:, :]), reads=[tmpfb], writes=[maskb])
    P.dma("sp", gT[:, :, :], norm_g.rearrange("l (c p) -> p l c", p=128), writes=[gTb], slow=True)
    P.dma("sp", scT[:, :, :], a_scale.rearrange("l (c p) -> p l c", p=128), writes=[scTb], slow=True)
    P.dma("sp", esk[:, :], b_sinks[0:1, :].partition_broadcast(128), writes=[eskb], slow=True)
    P.op("act", lambda e: e.activation(out=esk[:, :], in_=esk[:, :], func=AF.Exp), writes=[eskb])
    for t in range(16):
        P.op("pool", lambda e, t=t: e.memset(rc16[:, t:t + 1], 1.0 / (t + 1)), writes=[rcb])
    for gi in range(4):
        P.op("pool", lambda e, gi=gi: e.tensor_scalar(out=rcw[:, gi, :], in0=rc16[:, :],
                                                      scalar1=1.0 / (2 << gi), scalar2=None, op0=ALU.max),
             writes=[rcb])
    esk2 = A.alloc([128, NCH], F32)
    P.op("pool", lambda e: e.tensor_copy(out=esk2[0:64, :], in_=esk[0:64, 0:32:2]), reads=[eskb], writes=[eskb])
    P.op("pool", lambda e: e.tensor_copy(out=esk2[64:128, :], in_=esk[64:128, 1:32:2]), reads=[eskb], writes=[eskb])
    base_mark = A.mark()

    PB = (0, 1, 2)
    SBK = (3, 4, 5)
    OBK = (6, 7)
    pctr = [0]

    def next_pbank():
        b = PB[pctr[0] % 3]
        pctr[0] += 1
        return b

    class AttnPipe:
        DEPTH = 2

        def __init__(self, E, Eb, mask2):
            self.E, self.Eb, self.mask2 = E, Eb, mask2
            self.ns = 0
            self.pending = []

        def push(self, units, kreads, vreads, o_cb):
            sidx = self.ns
            self.ns += 1
            bi = SBK[sidx % 3]
            nu = len(units)
            for u, un in enumerate(units):
                kp = un["kprev"] if un["kprev"] is not None else un["kcur"]
                P.op("pe", lambda e, bi=bi, u=u, un=un, kp=kp: e.matmul(
                    banks[bi][:, u * 256:u * 256 + 128], kp, un["q2"][:, u, :], start=True, stop=True,
                    skip_group_check=True), reads=kreads, writes=[bankb[bi]], mark=False)
                P.op("pe", lambda e, bi=bi, u=u, un=un: e.matmul(
                    banks[bi][:, u * 256 + 128:u * 256 + 256], un["kcur"], un["q2"][:, u, :], start=True, stop=True,
                    skip_group_check=True), reads=kreads, writes=[bankb[bi]], mark=(u == nu - 1))
            ei = sidx % len(self.E)
            E_t, E_b = self.E[ei], self.Eb[ei]
            P.op("act", lambda e, bi=bi, E_t=E_t: e.activation(out=E_t[:, :], in_=banks[bi][:, :], func=AF.Exp,
                                                            scale=0.125),
                 reads=[bankb[bi]], writes=[E_b])
            P.op("dve", lambda e, E_t=E_t: e.tensor_tensor(
                out=E_t[:, :].rearrange("p (u c) -> p u c", u=2), in0=E_t[:, :].rearrange("p (u c) -> p u c", u=2),
                in1=self.mask2[:, :].unsqueeze(1).to_broadcast([128, 2, 256]), op=ALU.mult),
                reads=[maskb], writes=[E_b])
            self.pending.append((sidx, units, ei, vreads, o_cb))
            if len(self.pending) > self.DEPTH:
                self._pv(self.pending.pop(0))

        def _pv(self, item):
            sidx, units, ei, vreads, o_cb = item
            oi = OBK[(sidx // 2) % 2]
            half = sidx % 2
            E_t, E_b = self.E[ei], self.Eb[ei]
            nu = len(units)
            for u, un in enumerate(units):
                col = half * 256 + u * 128
                last = (u == nu - 1)
                if un["vprev"] is not None:
                    P.op("pe", lambda e, oi=oi, col=col, u=u, un=un, E_t=E_t: e.matmul(
                        banks[oi][:, col:col + 128], un["vprev"], E_t[:, u * 256:u * 256 + 128], start=True,
                        stop=False, skip_group_check=True),
                        reads=[E_b] + vreads, writes=[bankb[oi]], mark=False)
                    P.op("pe", lambda e, oi=oi, col=col, u=u, un=un, E_t=E_t: e.matmul(
                        banks[oi][:, col:col + 128], un["vcur"], E_t[:, u * 256 + 128:u * 256 + 256], start=False,
                        stop=True, skip_group_check=True),
                        reads=[E_b] + vreads, writes=[bankb[oi]], mark=last)
                else:
                    P.op("pe", lambda e, oi=oi, col=col, u=u, un=un, E_t=E_t: e.matmul(
                        banks[oi][:, col:col + 128], un["vcur"], E_t[:, u * 256 + 128:u * 256 + 256], start=True,
                        stop=True, skip_group_check=True),
                        reads=[E_b] + vreads, writes=[bankb[oi]], mark=last)
            if o_cb is not None:
                o_cb(oi)

        def flush(self):
            while self.pending:
                self._pv(self.pending.pop(0))

    def finalize(UA, UB, Ub, rec, recb, sg_t, sg_b, zs_t, zs_b, N, sinkcol):
        f0 = AF.Ln if sinkcol is None else AF.Copy
        P.op("act", lambda e: e.activation(out=rec[0:64, 0:N], in_=UA[64:128, 0:N], func=f0),
             reads=[Ub], writes=[recb])
        P.op("act", lambda e: e.activation(out=rec[64:128, 0:N], in_=UB[0:64, 0:N], func=f0),
             reads=[Ub], writes=[recb])
        if sinkcol is not None:
            P.op("dve", lambda e: e.tensor_scalar(out=rec[:, 0:N], in0=rec[:, 0:N],
                                                  scalar1=esk2[:, sinkcol:sinkcol + 1], scalar2=None, op0=ALU.add),
                 reads=[eskb], writes=[recb])
            P.op("act", lambda e: e.activation(out=rec[:, 0:N], in_=rec[:, 0:N], func=AF.Ln), writes=[recb])
        P.op("act", lambda e: e.activation(out=rec[:, 0:N], in_=rec[:, 0:N], func=AF.Exp, scale=-1.0),
             writes=[recb])
        P.op("pool", lambda e: e.tensor_tensor(out=rec[0:64, 0:N], in0=UA[0:64, 0:N], in1=rec[0:64, 0:N],
                                               op=ALU.mult), reads=[Ub], writes=[recb])
        P.op("pool", lambda e: e.tensor_tensor(out=rec[64:128, 0:N], in0=UB[64:128, 0:N], in1=rec[64:128, 0:N],
                                               op=ALU.mult), reads=[Ub], writes=[recb])
        P.op("pool", lambda e: e.tensor_tensor(out=zs_t[:, 0:N], in0=rec[:, 0:N], in1=sg_t[:, 0:N],
                                               op=ALU.mult), reads=[recb, sg_b], writes=[zs_b])

    def proj_fm(w_ap_k, h_t, cols, reads, evac):
        b = next_pbank()
        for k in range(KD):
            P.op("pe", lambda e, b=b, k=k: e.matmul(banks[b][:, :], w_ap_k(k), h_t[:, k, cols[0]:cols[1]],
                                                   start=(k == 0), stop=(k == KD - 1), skip_group_check=True),
                 reads=reads, writes=[bankb[b]], mark=(k == KD - 1))
        evac(b)

    def phase_swa(l, j):
        m0 = A.mark()
        NT = T // 512
        wkd = [A.alloc([128, KD, 128], BF16) for _ in range(2)]
        wv = [A.alloc([128, KD, 64], BF16) for _ in range(2)]
        wq = [A.alloc([128, KD, 512], BF16) for _ in range(2)]
        wgt = [A.alloc([128, KD, 512], BF16) for _ in range(2)]
        wb = [Buf() for _ in range(2)]
        hts = [A.alloc([128, KD, 512], BF16) for _ in range(3)]
        htsb = [Buf() for _ in range(3)]
        kT = [A.alloc([128, 640], BF16) for _ in range(2)]
        kTb = [Buf() for _ in range(2)]
        vx = [A.alloc([128, 5, 192], BF16) for _ in range(2)]
        vxb = [Buf() for _ in range(2)]
        qs2 = [A.alloc([128, 2, 512], BF16) for _ in range(2)]
        qsA = [t[:, 0, :] for t in qs2]
        qsB = [t[:, 1, :] for t in qs2]
        qsb = [Buf() for _ in range(2)]
        for m_ in range(2):
            P.op("pool", lambda e, m_=m_: e.memset(qs2[m_][:, :, :], 0.0), writes=[qsb[m_]])
        sgs = [A.alloc([128, 4, 512], BF16) for _ in range(2)]
        sgsb = [Buf() for _ in range(2)]
        UA = [A.alloc([128, 512], F32) for _ in range(2)]
        UB = [A.alloc([128, 512], F32) for _ in range(2)]
        Ub = [Buf() for _ in range(2)]
        rec = A.alloc([128, 512], F32)
        recb = Buf()
        zs = [A.alloc([128, 512], BF16) for _ in range(2)]
        zsb = [Buf() for _ in range(2)]
        E = [A.alloc([128, 512], BF16) for _ in range(4)]
        Eb = [Buf() for _ in range(4)]
        pipe = AttnPipe(E, Eb, mask_b)
        for s in range(2):
            P.op("pool", lambda e, s=s: e.memset(vx[s][:, :, :], 1.0), writes=[vxb[s]])

        def load_w(hk):
            s = hk % 2
            kc = DB + hk * 64
            vc = DB + 256 + hk * 64
            for dup in range(2):
                P.dma("pool", wkd[s][:, :, dup * 64:(dup + 1) * 64],
                      b_w_in[0, :, kc:kc + 64].rearrange("(k p) e -> p k e", p=128), writes=[wb[s]])
            P.dma("pool", wv[s][:, :, :], b_w_in[0, :, vc:vc + 64].rearrange("(k p) e -> p k e", p=128),
                  writes=[wb[s]])
            P.dma("pool", wq[s][:, :, :], b_w_in[0, :, hk * 512:(hk + 1) * 512].rearrange("(k p) e -> p k e", p=128),
                  writes=[wb[s]])
            gc = DB + 512 + hk * 512
            P.dma("pool", wgt[s][:, :, :], b_w_in[0, :, gc:gc + 512].rearrange("(k p) e -> p k e", p=128),
                  writes=[wb[s]])

        tiles = [(hk, tt) for hk in range(4) for tt in range(NT)]
        jobs = [(ti, cq) for ti in range(len(tiles)) for cq in range(4)]
        kT3 = kT + [A.alloc([128, 640], BF16)]
        kTb3 = kTb + [Buf()]
        vx3 = vx + [A.alloc([128, 5, 192], BF16)]
        vxb3 = vxb + [Buf()]
        P.op("pool", lambda e: e.memset(vx3[2][:, :, :], 1.0), writes=[vxb3[2]])

        def prologue(ti):
            hk, tt = tiles[ti]
            s = hk % 2
            if tt == 0 and hk + 1 < 4:
                load_w(hk + 1)
            h_t, h_b = hts[ti % 3], htsb[ti % 3]
            k_t, k_b = kT3[ti % 3], kTb3[ti % 3]
            v_t, v_b = vx3[ti % 3], vxb3[ti % 3]
            kn_t, kn_b = kT3[(ti + 1) % 3], kTb3[(ti + 1) % 3]
            vn_t, vn_b = vx3[(ti + 1) % 3], vxb3[(ti + 1) % 3]
            if ti + 1 < len(tiles):
                load_h(ti + 1)
            proj_fm(lambda k, s=s: wkd[s][:, k, :], h_t, (0, 512), [wb[s], h_b],
                    lambda b, k_t=k_t, k_b=k_b: P.op(
                        "act", lambda e: e.activation(out=k_t[:, 128:640], in_=banks[b][:, :], func=AF.Copy),
                        reads=[bankb[b]], writes=[k_b]))
            b = next_pbank()
            for blk in range(4):
                for k in range(KD):
                    P.op("pe", lambda e, b=b, blk=blk, k=k, h_t=h_t, s=s: e.matmul(
                        banks[b][:, blk * 64:(blk + 1) * 64], h_t[:, k, blk * 128:(blk + 1) * 128],
                        wv[s][:, k, :], start=(k == 0), stop=(k == KD - 1), skip_group_check=True),
                        reads=[wb[s], h_b], writes=[bankb[b]], mark=(k == KD - 1))
            P.op("dve", lambda e, b=b, v_t=v_t: e.tensor_copy(
                out=v_t[:, 1:5, 64:128], in_=banks[b][:, 0:256].rearrange("p (b e) -> p b e", b=4)),
                reads=[bankb[b]], writes=[v_b])
            if tt + 1 < NT:
                P.op("pool", lambda e, k_t=k_t, kn_t=kn_t: e.tensor_copy(out=kn_t[:, 0:128], in_=k_t[:, 512:640]),
                     reads=[k_b], writes=[kn_b])
                P.op("pool", lambda e, v_t=v_t, vn_t=vn_t: e.tensor_copy(out=vn_t[:, 0, 64:128],
                                                                      in_=v_t[:, 4, 64:128]),
                     reads=[v_b], writes=[vn_b])
            g4_t, g_b = sgs[ti % 2], sgsb[ti % 2]
            for cq in range(4):
                proj_fm(lambda k, s=s, cq=cq: wgt[s][:, k, cq * 128:(cq + 1) * 128], h_t, (0, 512), [wb[s], h_b],
                        lambda b, cq=cq, g4_t=g4_t, g_b=g_b: P.op(
                            "act", lambda e: e.activation(out=g4_t[:, cq, :], in_=banks[b][:, :], func=AF.Silu),
                            reads=[bankb[b]], writes=[g_b]))

        def qproj(jn):
            ti, cq = jobs[jn]
            hk, tt = tiles[ti]
            s = hk % 2
            h_t, h_b = hts[ti % 3], htsb[ti % 3]
            qa_t, qb_t, q_b = qsA[jn % 2], qsB[jn % 2], qsb[jn % 2]
            proj_fm(lambda k, s=s, cq=cq: wq[s][:, k, cq * 128:(cq + 1) * 128], h_t, (0, 512), [wb[s], h_b],
                    lambda b, qa_t=qa_t, qb_t=qb_t, q_b=q_b: (
                        P.op("act", lambda e: e.activation(out=qa_t[0:64, :], in_=banks[b][0:64, :], func=AF.Copy),
                             reads=[bankb[b]], writes=[q_b]),
                        P.op("act", lambda e: e.activation(out=qb_t[64:128, :], in_=banks[b][64:128, :],
                                                           func=AF.Copy),
                             reads=[bankb[b]], writes=[q_b])))

        def make_tail(jn):
            ti, cq = jobs[jn]
            hk, tt = tiles[ti]
            c = hk * 4 + cq
            m = jn % 2
            g_t, g_b = sgs[ti % 2][:, cq, :], sgsb[ti % 2]

            def do_tail():
                oz, dz, u_b = UA[m], UB[m], Ub[m]
                P.op("act", lambda e: e.activation(out=rec[:, :], in_=dz[:, :], func=AF.Ln,
                                                   bias=esk2[:, c:c + 1]), reads=[u_b, eskb], writes=[recb])
                P.op("act", lambda e: e.activation(out=rec[:, :], in_=rec[:, :], func=AF.Exp, scale=-1.0),
                     writes=[recb])
                P.op("pool", lambda e: e.tensor_tensor(out=rec[:, :], in0=oz[:, :], in1=rec[:, :], op=ALU.mult),
                     reads=[u_b], writes=[recb])
                P.op("pool", lambda e: e.tensor_tensor(out=zs[m][:, :], in0=rec[:, :], in1=g_t, op=ALU.mult),
                     reads=[recb, g_b], writes=[zsb[m]])
                P.dma("pool", zT[:, c, tt * 512:(tt + 1) * 512], zs[m][:, :], reads=[zsb[m]])
            return do_tail

        def attention(jn, prev_tail):
            ti, cq = jobs[jn]
            hk, tt = tiles[ti]
            m = jn % 2
            k_t, k_b = kT3[ti % 3], kTb3[ti % 3]
            v_t, v_b = vx3[ti % 3], vxb3[ti % 3]
            qa_t, qb_t, q_b = qsA[m], qsB[m], qsb[m]
            ua, ub_, u_b = UA[m], UB[m], Ub[m]
            for blk in range(4):
                has_prev = not (tt == 0 and blk == 0)
                units = []
                for hh in range(2):
                    vsl = (64, 192) if hh == 0 else (0, 128)
                    units.append(dict(
                        q2=qs2[m][:, :, blk * 128:(blk + 1) * 128],
                        kprev=k_t[:, blk * 128:(blk + 1) * 128] if has_prev else None,
                        kcur=k_t[:, (blk + 1) * 128:(blk + 2) * 128],
                        vprev=v_t[:, blk, vsl[0]:vsl[1]] if has_prev else None,
                        vcur=v_t[:, blk + 1, vsl[0]:vsl[1]]))
                cb = None
                if blk % 2 == 1:
                    def cb(oi, blk=blk, ua=ua, ub_=ub_, u_b=u_b):
                        c0 = (blk - 1) * 128
                        ov = banks[oi][:, :].rearrange("p (b h q) -> p b h q", b=2, h=2)
                        for (dst, r0, src_r0, hh) in ((ua, 0, 0, 0), (ua, 64, 64, 1), (ub_, 0, 64, 0), (ub_, 64, 0, 1)):
                            P.op("dve", lambda e, dst=dst, r0=r0, src_r0=src_r0, hh=hh: e.tensor_copy(
                                out=dst[r0:r0 + 64, c0:c0 + 256].rearrange("p (b q) -> p b q", b=2),
                                in_=ov[src_r0:src_r0 + 64, :, hh, :]),
                                reads=[bankb[oi]], writes=[u_b])
                pipe.push(units, [k_b, q_b], [v_b], cb)
            if prev_tail is not None:
                prev_tail()

        def load_h(ti):
            tt = tiles[ti][1]
            P.dma("sp", hts[ti % 3][:, :, :], hT[:, :, tt * 512:(tt + 1) * 512], writes=[htsb[ti % 3]])

        load_w(0)
        load_h(0)
        prologue(0)
        qproj(0)
        prev_tail = None
        for jn in range(len(jobs)):
            if jn + 1 < len(jobs):
                if jobs[jn + 1][1] == 0:
                    prologue(jobs[jn + 1][0])
                qproj(jn + 1)
            attention(jn, prev_tail)
            prev_tail = make_tail(jn)
        pipe.flush()
        prev_tail()
        P.barrier()
        A.release(m0)

    def phase_dil(l, j):
        m0 = A.mark()
        NSB = T // 2048
        DIL = (1, 4, 16)
        wq = [A.alloc([128, KD, 128], BF16) for _ in range(3)]
        wk = [A.alloc([128, KD, 128], BF16) for _ in range(3)]
        wv = [A.alloc([128, KD, 128], BF16) for _ in range(3)]
        wgb = [Buf() for _ in range(3)]
        wgt = A.alloc([128, KD, 128], BF16)
        wgtb = Buf()
        hts = [A.alloc([128, KD, 2048], BF16) for _ in range(2)]
        htsb = [Buf() for _ in range(2)]
        kT = [A.alloc([128, 2, 2048], BF16) for _ in range(3)]
        kTb = [[Buf() for _ in range(2)] for _ in range(3)]
        vx = [A.alloc([128, 2, 16, 192], BF16) for _ in range(3)]
        vxb = [[Buf() for _ in range(2)] for _ in range(3)]
        qT2 = [A.alloc([128, 2, 2048], BF16) for _ in range(2)]
        qTA = [t[:, 0, :] for t in qT2]
        qTB = [t[:, 1, :] for t in qT2]
        qTb = [Buf() for _ in range(2)]
        for m_ in range(2):
            P.op("pool", lambda e, m_=m_: e.memset(qT2[m_][:, :, :], 0.0), writes=[qTb[m_]])
        sgs_ = [A.alloc([128, 2048], BF16) for _ in range(2)]
        sgbs_ = [Buf() for _ in range(2)]
        UA = A.alloc([128, 2048], F32)
        UB = A.alloc([128, 2048], F32)
        Ub = Buf()
        rec = A.alloc([128, 2048], F32)
        recb = Buf()
        zs = A.alloc([128, 2048], BF16)
        zsb = Buf()
        E = [A.alloc([128, 512], BF16) for _ in range(4)]
        Eb = [Buf() for _ in range(4)]
        pipe = AttnPipe(E, Eb, mask_c)
        for g in range(3):
            P.op("pool", lambda e, g=g: e.memset(vx[g][:, :, :, 64:128], 1.0), writes=[vxb[g][0], vxb[g][1]])

        def wcols(idx, c):
            return c_w_in[0, :, idx * DB + c * 128:idx * DB + (c + 1) * 128].rearrange("(k p) e -> p k e", p=128)

        def load_wg(c, g):
            P.dma("pool", wq[g][:, :, :], wcols(3 * g + 0, c), writes=[wgb[g]])
            P.dma("pool", wk[g][:, :, :], wcols(3 * g + 1, c), writes=[wgb[g]])
            P.dma("pool", wv[g][:, :, :], wcols(3 * g + 2, c), writes=[wgb[g]])

        def load_wgate(c):
            P.dma("pool", wgt[:, :, :], wcols(9, c), writes=[wgtb])

        load_wgate(0)
        for g in range(3):
            load_wg(0, g)
        n = 0
        qn = 0
        tailc = [None]
        for c in range(NCH):
            for sb in range(NSB):
                if tailc[0] is not None:
                    tailc[0]()
                    tailc[0] = None
                h_t, h_b = hts[n % 2], htsb[n % 2]
                sg, sgb = sgs_[n % 2], sgbs_[n % 2]
                n += 1
                half = sb % 2
                for q4 in range(4):
                    P.dma("sp", h_t[:, :, q4 * 512:(q4 + 1) * 512],
                          hT[:, :, sb * 2048 + q4 * 512:sb * 2048 + (q4 + 1) * 512], writes=[h_b])
                for tl in range(4):
                    proj_fm(lambda k: wgt[:, k, :], h_t, (tl * 512, (tl + 1) * 512), [wgtb, h_b],
                            lambda b, tl=tl, sg=sg, sgb=sgb: P.op(
                                "act", lambda e: e.activation(out=sg[:, tl * 512:(tl + 1) * 512], in_=banks[b][:, :],
                                                              func=AF.Silu),
                                reads=[bankb[b]], writes=[sgb]))
                if sb == NSB - 1 and c + 1 < NCH:
                    load_wgate(c + 1)
                for g in range(3):
                    d = DIL[g]
                    nI = 16 // d
                    qa_t, qb_t, q_b = qTA[qn % 2], qTB[qn % 2], qTb[qn % 2]
                    q2_t = qT2[qn % 2]
                    qn += 1
                    k_t = kT[g]
                    v_t = vx[g]
                    for tl in range(4):
                        proj_fm(lambda k, g=g: wk[g][:, k, :], h_t, (tl * 512, (tl + 1) * 512), [wgb[g], h_b],
                                lambda b, tl=tl, k_t=k_t, g=g, half=half: P.op(
                                    "act", lambda e: e.activation(out=k_t[:, half, tl * 512:(tl + 1) * 512],
                                                                  in_=banks[b][:, :], func=AF.Copy),
                                    reads=[bankb[b]], writes=[kTb[g][half]]))
                    for tl in range(4):
                        proj_fm(lambda k, g=g: wq[g][:, k, :], h_t, (tl * 512, (tl + 1) * 512), [wgb[g], h_b],
                                lambda b, tl=tl, qa_t=qa_t, qb_t=qb_t, q_b=q_b: (
                                    P.op("dve", lambda e: e.tensor_copy(out=qa_t[0:64, tl * 512:(tl + 1) * 512],
                                                                        in_=banks[b][0:64, :]),
                                         reads=[bankb[b]], writes=[q_b]),
                                    P.op("dve", lambda e: e.tensor_copy(out=qb_t[64:128, tl * 512:(tl + 1) * 512],
                                                                        in_=banks[b][64:128, :]),
                                         reads=[bankb[b]], writes=[q_b])))
                    def blk_start(bk):
                        i, r = bk // d, bk % d
                        return i * 128 * d + r
                    for b4 in range(4):
                        b = next_pbank()
                        for bb in range(4):
                            bk = b4 * 4 + bb
                            st = blk_start(bk)
                            for k in range(KD):
                                P.op("pe", lambda e, b=b, bb=bb, st=st, k=k, g=g, d=d, h_t=h_t: e.matmul(
                                    banks[b][:, bb * 128:(bb + 1) * 128], h_t[:, k, st:st + 127 * d + 1:d],
                                    wv[g][:, k, :], start=(k == 0), stop=(k == KD - 1), skip_group_check=True),
                                    reads=[wgb[g], h_b], writes=[bankb[b]], mark=(k == KD - 1))
                        P.op("dve", lambda e, b=b, b4=b4, v_t=v_t, half=half: e.tensor_copy(
                            out=v_t[:, half, b4 * 4:(b4 + 1) * 4, :].rearrange("p b (t e) -> p b t e", t=3)[:, :, 0:3:2, :],
                            in_=banks[b][:, :].rearrange("p (b t e) -> p b t e", b=4, t=2)),
                            reads=[bankb[b]], writes=[vxb[g][half]])
                    if sb == NSB - 1 and c + 1 < NCH:
                        load_wg(c + 1, g)
                    for bk in range(16):
                        i, r = bk // d, bk % d
                        st = blk_start(bk)
                        if i >= 1:
                            ph, pbk = half, (i - 1) * d + r
                            has_prev = True
                        else:
                            ph, pbk = 1 - half, (nI - 1) * d + r
                            has_prev = sb >= 1
                        pst = blk_start(pbk)
                        units = []
                        for hh in range(2):
                            r0, r1 = hh * 64, hh * 64 + 64
                            vsl = (0, 128) if hh == 0 else (64, 192)
                            units.append(dict(
                                q2=q2_t[:, :, st:st + 127 * d + 1:d],
                                kprev=k_t[:, ph, pst:pst + 127 * d + 1:d] if has_prev else None,
                                kcur=k_t[:, half, st:st + 127 * d + 1:d],
                                vprev=v_t[:, ph, pbk, vsl[0]:vsl[1]] if has_prev else None,
                                vcur=v_t[:, half, bk, vsl[0]:vsl[1]]))
                        cb = None
                        if bk % 2 == 1:
                            def cb(oi, bk=bk, g=g, d=d, st0=blk_start(bk - 1)):
                                bstr = 1 if d > 1 else 128
                                ov = banks[oi][:, :].rearrange("p (b h q) -> p b h q", b=2, h=2)
                                for hh, U in ((0, UA), (1, UB)):
                                    uv = bass.AP(U, st0, [[2048, 128], [bstr, 2], [d, 128]])
                                    if g == 0:
                                        P.op("dve", lambda e, uv=uv, hh=hh: e.tensor_copy(out=uv, in_=ov[:, :, hh, :]),
                                             reads=[bankb[oi]], writes=[Ub])
                                    else:
                                        P.op("dve", lambda e, uv=uv, hh=hh: e.tensor_tensor(
                                            out=uv, in0=ov[:, :, hh, :], in1=uv, op=ALU.add),
                                            reads=[bankb[oi]], writes=[Ub])
                        kr = [kTb[g][half], q_b] + ([kTb[g][ph]] if has_prev and ph != half else [])
                        vr = [vxb[g][half]] + ([vxb[g][ph]] if has_prev and ph != half else [])
                        pipe.push(units, kr, vr, cb)

                def do_tail(sg=sg, sgb=sgb, c=c, sb=sb):
                    pipe.flush()
                    finalize(UA, UB, Ub, rec, recb, sg, sgb, zs, zsb, 2048, None)
                    P.dma("pool", zT[:, c, sb * 2048:(sb + 1) * 2048], zs[:, :], reads=[zsb])
                tailc[0] = do_tail
        tailc[0]()
        P.barrier()
        A.release(m0)

    def phase_norm(x_src, l):
        m0 = A.mark()
        xt = [A.alloc([128, DM], F32) for _ in range(3)]
        xtb = [Buf() for _ in range(3)]
        sq = A.alloc([128, DM], F32)
        sqb = Buf()
        ss = [A.alloc([128, 2], F32) for _ in range(2)]
        ssb = [Buf() for _ in range(2)]
        hb = [A.alloc([128, DM], BF16) for _ in range(2)]
        hbb = [Buf() for _ in range(2)]
        hts = [A.alloc([128, KD, 512], BF16) for _ in range(2)]
        htsb = [Buf() for _ in range(2)]
        for i in range(T // 128):
            x_t, x_b = xt[i % 3], xtb[i % 3]
            s_t, s_b = ss[i % 2], ssb[i % 2]
            h_t, h_b = hb[i % 2], hbb[i % 2]
            o_t, o_b = hts[(i // 4) % 2], htsb[(i // 4) % 2]
            bk, bkb = banks[i % 2], bankb[i % 2]
            P.dma("sp", x_t[:, :], x_src[i * 128:(i + 1) * 128, :], writes=[x_b])
            P.op("act", lambda e, x_t=x_t: e.activation(out=sq[:, :], in_=x_t[:, :], func=AF.Square),
                 reads=[x_b], writes=[sqb])
            P.op("dve", lambda e, s_t=s_t: e.reduce_sum(out=s_t[:, 0:1], in_=sq[:, :], axis=AX.X),
                 reads=[sqb], writes=[s_b])
            P.op("act", lambda e, s_t=s_t: e.activation(out=s_t[:, 1:2], in_=s_t[:, 0:1], func=AF.Sqrt,
                                                         bias=epst[:, 0:1], scale=1.0 / DM),
                 reads=[epsb], writes=[s_b])
            P.op("dve", lambda e, s_t=s_t: e.reciprocal(out=s_t[:, 0:1], in_=s_t[:, 1:2]), writes=[s_b])
            P.op("act", lambda e, x_t=x_t, s_t=s_t, h_t=h_t: e.activation(
                out=h_t[:, :], in_=x_t[:, :], func=AF.Copy, scale=s_t[:, 0:1]),
                reads=[x_b, s_b], writes=[h_b])
            pT = bk[:, :].bitcast(BF16)
            for c in range(KD):
                P.op("pe", lambda e, c=c, pT=pT, h_t=h_t: e.transpose(
                    out=pT[:, c * 128:(c + 1) * 128], in_=h_t[:, c * 128:(c + 1) * 128], identity=ident[:, :]),
                    reads=[h_b, identb], writes=[bkb], mark=(c == KD - 1))
            sl = (i % 4) * 128
            P.op("dve", lambda e, pT=pT, o_t=o_t, sl=sl: e.tensor_tensor(
                out=o_t[:, :, sl:sl + 128], in0=pT.rearrange("p (c t) -> p c t", c=KD),
                in1=gT[:, l, :].unsqueeze(2).to_broadcast([128, KD, 128]), op=ALU.mult),
                reads=[bkb, gTb], writes=[o_b])
            if i % 4 == 3:
                tt = i // 4
                P.dma("pool", hT[:, :, tt * 512:(tt + 1) * 512], o_t[:, :, :], reads=[o_b])
        P.barrier()
        A.release(m0)

    def phase_out(x_src, x_dst, l, last, next_l=None):
        m0 = A.mark()
        if next_l is not None:
            nsq = A.alloc([128, DM], F32)
            nsqb = Buf()
            nss = [A.alloc([128, 2], F32) for _ in range(3)]
            nssb = [Buf() for _ in range(3)]
            nhb = [A.alloc([128, DM], BF16) for _ in range(3)]
            nhbb = [Buf() for _ in range(3)]
            nhts = [A.alloc([128, KD, 512], BF16) for _ in range(2)]
            nhtsb = [Buf() for _ in range(2)]
        wo = A.alloc([128, NCH, DM], BF16)
        wobs = [Buf() for _ in range(NCH)]
        for c in range(NCH):
            P.dma("pool", wo[:, c, :], w_out[l, c * 128:(c + 1) * 128, :], writes=[wobs[c]])
        zt = [A.alloc([128, NCH, 512], BF16) for _ in range(2)]
        ztb = [Buf() for _ in range(2)]
        xt = [A.alloc([128, DM], F32) for _ in range(3)]
        xtb = [Buf() for _ in range(3)]
        xo = [A.alloc([128, DM], F32) for _ in range(2)]
        xob = [Buf() for _ in range(2)]
        if last:
            gf = A.alloc([128, DM], F32)
            gfb = Buf()
            P.dma("sp", gf[:, :], final_g[0:1, :].partition_broadcast(128), writes=[gfb], slow=True)
            sq = A.alloc([128, DM], F32)
            sqb = Buf()
            ss = [A.alloc([128, 2], F32) for _ in range(2)]
            ssb = [Buf() for _ in range(2)]
            yo = [A.alloc([128, DM], F32) for _ in range(2)]
            yob = [Buf() for _ in range(2)]
        pend_tr = []
        for i in range(T // 128):
            tt, s = i // 4, i % 4
            z_t, z_b = zt[tt % 2], ztb[tt % 2]
            if s == 0:
                P.dma("sp", z_t[:, :, :], zT[:, :, tt * 512:(tt + 1) * 512], writes=[z_b])
            x_t, x_b = xt[i % 3], xtb[i % 3]
            o_t, o_b = xo[i % 2], xob[i % 2]
            P.dma("sp", x_t[:, :], x_src[i * 128:(i + 1) * 128, :], writes=[x_b])
            b0 = 4 + 2 * (i % 2)
            for c in range(NCH):
                for half in range(2):
                    P.op("pe", lambda e, c=c, half=half, b0=b0, z_t=z_t, s=s: e.matmul(
                        banks[b0 + half][:, :], z_t[:, c, s * 128:(s + 1) * 128],
                        wo[:, c, half * 512:(half + 1) * 512], start=(c == 0), stop=(c == NCH - 1),
                        skip_group_check=True),
                        reads=[z_b, wobs[c]], writes=[bankb[b0 + half]], mark=(c == NCH - 1))
            if len(pend_tr) >= 2:
                pend_tr.pop(0)()
            for half in range(2):
                P.op("dve", lambda e, half=half, b0=b0, x_t=x_t, o_t=o_t: e.tensor_tensor(
                    out=o_t[:, half * 512:(half + 1) * 512], in0=banks[b0 + half][:, :],
                    in1=x_t[:, half * 512:(half + 1) * 512], op=ALU.add),
                    reads=[bankb[b0 + half], x_b], writes=[o_b])
            if not last:
                P.dma("pool", x_dst[i * 128:(i + 1) * 128, :], o_t[:, :], reads=[o_b])
                if next_l is not None:
                    s_t, s_b = nss[i % 3], nssb[i % 3]
                    h_t, h_b = nhb[i % 3], nhbb[i % 3]
                    ho_t, ho_b = nhts[(i // 4) % 2], nhtsb[(i // 4) % 2]
                    bk, bkb = banks[i % 2], bankb[i % 2]
                    P.op("act", lambda e, o_t=o_t: e.activation(out=nsq[:, :], in_=o_t[:, :], func=AF.Square),
                         reads=[o_b], writes=[nsqb])
                    P.op("dve", lambda e, s_t=s_t: e.reduce_sum(out=s_t[:, 0:1], in_=nsq[:, :], axis=AX.X),
                         reads=[nsqb], writes=[s_b])
                    P.op("act", lambda e, s_t=s_t: e.activation(out=s_t[:, 1:2], in_=s_t[:, 0:1], func=AF.Sqrt,
                                                                 bias=epst[:, 0:1], scale=1.0 / DM),
                         reads=[epsb], writes=[s_b])
                    P.op("dve", lambda e, s_t=s_t: e.reciprocal(out=s_t[:, 0:1], in_=s_t[:, 1:2]), writes=[s_b])
                    P.op("act", lambda e, o_t=o_t, s_t=s_t, h_t=h_t: e.activation(
                        out=h_t[:, :], in_=o_t[:, :], func=AF.Copy, scale=s_t[:, 0:1]),
                        reads=[o_b, s_b], writes=[h_b])
                    def do_tr(i=i, h_t=h_t, h_b=h_b, ho_t=ho_t, ho_b=ho_b, bk=bk, bkb=bkb):
                        pT = bk[:, :].bitcast(BF16)
                        for c in range(KD):
                            P.op("pe", lambda e, c=c, pT=pT, h_t=h_t: e.transpose(
                                out=pT[:, c * 128:(c + 1) * 128], in_=h_t[:, c * 128:(c + 1) * 128],
                                identity=ident[:, :]),
                                reads=[h_b, identb], writes=[bkb], mark=(c == KD - 1))
                        sl = (i % 4) * 128
                        P.op("dve", lambda e, pT=pT, ho_t=ho_t, sl=sl: e.tensor_tensor(
                            out=ho_t[:, :, sl:sl + 128], in0=pT.rearrange("p (c t) -> p c t", c=KD),
                            in1=gT[:, next_l, :].unsqueeze(2).to_broadcast([128, KD, 128]), op=ALU.mult),
                            reads=[bkb, gTb], writes=[ho_b])
                        if i % 4 == 3:
                            P.dma("pool", hT[:, :, (i // 4) * 512:(i // 4 + 1) * 512], ho_t[:, :, :], reads=[ho_b])
                    pend_tr.append(do_tr)
            else:
                s_t, s_b = ss[i % 2], ssb[i % 2]
                y_t, y_b = yo[i % 2], yob[i % 2]
                P.op("act", lambda e, o_t=o_t: e.activation(out=sq[:, :], in_=o_t[:, :], func=AF.Square),
                     reads=[o_b], writes=[sqb])
                P.op("dve", lambda e, s_t=s_t: e.reduce_sum(out=s_t[:, 0:1], in_=sq[:, :], axis=AX.X),
                     reads=[sqb], writes=[s_b])
                P.op("act", lambda e, s_t=s_t: e.activation(out=s_t[:, 1:2], in_=s_t[:, 0:1], func=AF.Sqrt,
                                                             bias=epst[:, 0:1], scale=1.0 / DM),
                     reads=[epsb], writes=[s_b])
                P.op("dve", lambda e, s_t=s_t: e.reciprocal(out=s_t[:, 0:1], in_=s_t[:, 1:2]), writes=[s_b])
                P.op("dve", lambda e, s_t=s_t, o_t=o_t, y_t=y_t: e.scalar_tensor_tensor(
                    out=y_t[:, :], in0=o_t[:, :], scalar=s_t[:, 0:1], in1=gf[:, :], op0=ALU.mult, op1=ALU.mult),
                    reads=[o_b, s_b, gfb], writes=[y_b])
                P.dma("pool", x_dst[i * 128:(i + 1) * 128, :], y_t[:, :], reads=[y_b])
        while pend_tr:
            pend_tr.pop(0)()
        P.barrier()
        A.release(m0)

    def phase_pool(l, j):
        m0 = A.mark()
        wu = [A.alloc([128, KD, 512], BF16) for _ in range(2)]
        wg = [A.alloc([128, KD, 512], BF16) for _ in range(2)]
        wp = [A.alloc([128, 4, 512], BF16) for _ in range(2)]
        wb = [Buf() for _ in range(2)]
        hts = [A.alloc([128, KD, 512], BF16) for _ in range(3)]
        htsb = [Buf() for _ in range(3)]
        ub = [A.alloc([128, 4, 528], F32) for _ in range(2)]
        ubb = [Buf() for _ in range(2)]
        ta = A.alloc([128, 4, 528], F32)
        tb = A.alloc([128, 4, 528], F32)
        tab, tbb = Buf(), Buf()
        dbs = [A.alloc([128, 4, 512], BF16) for _ in range(2)]
        dbbs = [Buf() for _ in range(2)]
        t16 = A.alloc([128, 4, 16], F32)
        t16b = Buf()
        sg = [A.alloc([128, 2, 512], BF16) for _ in range(2)]
        sgb = [Buf() for _ in range(2)]
        zs = [A.alloc([128, 4, 512], BF16) for _ in range(2)]
        zsb = [Buf() for _ in range(2)]
        NT = T // 512
        steps = [(gi, tt) for gi in range(4) for tt in range(NT)]

        def load_w(gi):
            s = gi % 2
            P.dma("pool", wu[s][:, :, :], a_w_in[j, :, gi * 512:(gi + 1) * 512].rearrange("(k p) e -> p k e", p=128),
                  writes=[wb[s]])
            P.dma("pool", wg[s][:, :, :],
                  a_w_in[j, :, DB + gi * 512:DB + (gi + 1) * 512].rearrange("(k p) e -> p k e", p=128),
                  writes=[wb[s]])
            P.dma("pool", wp[s][:, :, :], a_w_group[j, gi, :, :].rearrange("(k p) e -> p k e", p=128),
                  writes=[wb[s]])

        def emit_u(n):
            gi, tt = steps[n]
            s = gi % 2
            h_t, h_b = hts[n % 3], htsb[n % 3]
            u_t, u_b = ub[n % 2], ubb[n % 2]
            for c in range(4):
                for k in range(KD):
                    P.op("pe", lambda e, c=c, k=k, s=s, h_t=h_t: e.matmul(
                        banks[c][:, :], wu[s][:, k, c * 128:(c + 1) * 128], h_t[:, k, :],
                        start=(k == 0), stop=(k == KD - 1), skip_group_check=True),
                        reads=[wb[s], h_b], writes=[bankb[c]], mark=(k == KD - 1))
            if tt == 0:
                P.op("pool", lambda e, u_t=u_t: e.memset(u_t[:, :, 0:16], 0.0), writes=[u_b])
            else:
                up_t, up_b = ub[(n - 1) % 2], ubb[(n - 1) % 2]
                P.op("act", lambda e, u_t=u_t, up_t=up_t: e.activation(out=u_t[:, :, 0:16], in_=up_t[:, :, 512:528],
                                                                    func=AF.Copy),
                     reads=[up_b], writes=[u_b])
            for c in range(4):
                P.op("act", lambda e, c=c, u_t=u_t: e.activation(out=u_t[:, c, 16:528], in_=banks[c][:, :],
                                                                func=AF.Copy),
                     reads=[bankb[c]], writes=[u_b])

        def emit_pool(n):
            gi, tt = steps[n]
            u_t, u_b = ub[n % 2], ubb[n % 2]
            db, dbb = dbs[n % 2], dbbs[n % 2]
            src, srcb = u_t, u_b
            lo = 0
            tmps = [(ta, tab), (tb, tbb)]
            for lvl in range(gi + 1):
                sh = 1 << lvl
                lo = lo + sh
                dst, dstb = tmps[lvl % 2]
                P.op("dve", lambda e, src=src, dst=dst, lo=lo, sh=sh: e.tensor_tensor(
                    out=dst[:, :, lo:528], in0=src[:, :, lo:528], in1=src[:, :, lo - sh:528 - sh], op=ALU.add),
                    reads=[srcb], writes=[dstb])
                src, srcb = dst, dstb
            w = 2 << gi
            P.op("dve", lambda e, src=src, u_t=u_t, w=w: e.scalar_tensor_tensor(
                out=db[:, :, :], in0=src[:, :, 16:528], scalar=1.0 / w, in1=u_t[:, :, 16:528],
                op0=ALU.mult, op1=ALU.subtract), reads=[srcb, u_b], writes=[dbb])
            if tt == 0:
                P.op("dve", lambda e, src=src, gi=gi: e.tensor_tensor(
                    out=t16[:, :, :], in0=src[:, :, 16:32],
                    in1=rcw[:, gi, :].unsqueeze(1).to_broadcast([128, 4, 16]), op=ALU.mult),
                    reads=[srcb, rcb], writes=[t16b])
                P.op("dve", lambda e, u_t=u_t: e.tensor_tensor(
                    out=db[:, :, 0:16], in0=t16[:, :, :], in1=u_t[:, :, 16:32], op=ALU.subtract),
                    reads=[t16b, u_b], writes=[dbb])

        def emit_rest(n):
            gi, tt = steps[n]
            s = gi % 2
            h_t, h_b = hts[n % 3], htsb[n % 3]
            z_t, z_b = zs[n % 2], zsb[n % 2]
            db, dbb = dbs[n % 2], dbbs[n % 2]
            for half in range(2):
                g_t, g_b = sg[half], sgb[half]
                for cc in range(2):
                    co = half * 2 + cc
                    for k in range(4):
                        P.op("pe", lambda e, co=co, k=k, s=s, cc=cc: e.matmul(
                            banks[4 + cc][:, :], wp[s][:, k, co * 128:(co + 1) * 128], db[:, k, :],
                            start=(k == 0), stop=(k == 3), skip_group_check=True),
                            reads=[wb[s], dbb], writes=[bankb[4 + cc]], mark=(k == 3))
                for cc in range(2):
                    co = half * 2 + cc
                    for k in range(KD):
                        P.op("pe", lambda e, co=co, k=k, s=s, cc=cc, h_t=h_t: e.matmul(
                            banks[6 + cc][:, :], wg[s][:, k, co * 128:(co + 1) * 128], h_t[:, k, :],
                            start=(k == 0), stop=(k == KD - 1), skip_group_check=True),
                            reads=[wb[s], h_b], writes=[bankb[6 + cc]], mark=(k == KD - 1))
                for cc in range(2):
                    P.op("act", lambda e, cc=cc, g_t=g_t: e.activation(out=g_t[:, cc, :], in_=banks[6 + cc][:, :],
                                                                    func=AF.Silu),
                         reads=[bankb[6 + cc]], writes=[g_b])
                for cc in range(2):
                    co = half * 2 + cc
                    ch = gi * 4 + co
                    P.op("dve", lambda e, cc=cc, co=co, ch=ch, g_t=g_t, z_t=z_t: e.scalar_tensor_tensor(
                        out=z_t[:, co, :], in0=banks[4 + cc][:, :], scalar=scT[:, j, ch:ch + 1],
                        in1=g_t[:, cc, :], op0=ALU.mult, op1=ALU.mult),
                        reads=[bankb[4 + cc], g_b, scTb], writes=[z_b])
            P.dma("pool", zT[:, gi * 4:(gi + 1) * 4, tt * 512:(tt + 1) * 512], z_t[:, :, :], reads=[z_b])

        def load_h(n):
            tt = steps[n][1]
            P.dma("sp", hts[n % 3][:, :, :], hT[:, :, tt * 512:(tt + 1) * 512], writes=[htsb[n % 3]])

        load_w(0)
        load_h(0)
        load_h(1)
        emit_u(0)
        emit_pool(0)
        for n in range(len(steps)):
            gi, tt = steps[n]
            if tt == 0 and gi + 1 < 4:
                load_w(gi + 1)
            if n + 2 < len(steps):
                load_h(n + 2)
            if n + 1 < len(steps):
                emit_u(n + 1)
                emit_pool(n + 1)
            emit_rest(n)
        P.barrier()
        A.release(m0)

    cur = x_in
    nl = len(layers)
    for li, l in enumerate(layers):
        last = (li == nl - 1) and final_norm
        dst = y_out if li == nl - 1 else xs[li % 2]
        kind, j = l % 3, l // 3
        if li == 0:
            phase_norm(cur, l)
        if kind == 0:
            phase_pool(l, j)
        elif kind == 1:
            phase_swa(l, j)
        else:
            phase_dil(l, j)
        phase_out(cur, dst, l, last, next_l=(layers[li + 1] if li + 1 < nl else None))
        cur = dst

    with nc.Block() as block:
        P.run_all(block)
    es.close()
    return nc


_INPUT_NAMES = ("norm_g", "final_g", "w_out", "a_w_in", "a_w_group", "a_scale", "b_w_in", "b_sinks", "c_w_in")


def _in_maps(inputs, x_per_core):
    maps = []
    shared = {}
    for k in _INPUT_NAMES:
        v = np.ascontiguousarray(np.asarray(inputs[k], dtype=np.float32))
        if k == "final_g":
            v = v.reshape(1, DM)
        shared[k] = v
    for c in range(len(x_per_core)):
        m = dict(shared)
        m["x"] = np.ascontiguousarray(x_per_core[c])
        maps.append(m)
    return maps


def kernel(**inputs):
    x = np.asarray(inputs["x"], dtype=np.float32)
    nc = build_program()
    maps = _in_maps(inputs, [x[b] for b in range(N_CORES)])
    res = run_bass_kernel_spmd(nc, maps, core_ids=list(range(N_CORES)))
    return np.stack([np.asarray(res.results[b]["out"], dtype=np.float32) for b in range(N_CORES)], axis=0)
```
